# Optimizing a Trainium2 kernel written in Bass

```python
import jax, jax.numpy as jnp
from jax import lax
import numpy as np

D_MODEL = 2048
BATCH = 8
SEQ = 2048
DEPTH = 4
DEC_BATCH = 8
DEC_SEQ = 32
PAST_LEN = 2048

CHUNK = 64
N_MIXERS = 3
EPS = 1e-6
D_FF = 4 * D_MODEL
CONV_WIDTH = 31
CONV_STATE = CONV_WIDTH - 1
FOX_HEADS = 16
FOX_HEAD_DIM = D_MODEL // FOX_HEADS
Q_BLOCK = 128
GLA_HEADS = 4
GLA_KEY_DIM = D_MODEL // 2 // GLA_HEADS
GLA_VAL_DIM = D_MODEL // GLA_HEADS
GLA_QK_WIDTH = GLA_HEADS * GLA_KEY_DIM
GLA_GATE_RANK = 16
GLA_TAU = 16.0

kernel_name = "hybrid_streaming_encoder_step"


def rms_norm(x, g):
    xf = x.astype(jnp.float32)
    y = xf * lax.rsqrt(jnp.mean(jnp.square(xf), axis=-1, keepdims=True) + EPS)
    return (y * g.astype(jnp.float32)).astype(x.dtype)


def sqrelu_mlp(h, w_up, w_down):
    return jnp.square(jax.nn.relu(h @ w_up)) @ w_down


def conv_module(h, hist, w_in, w_dw, g_norm, w_out):
    a, b = jnp.split(h @ w_in, 2, axis=-1)
    u = a * jax.nn.sigmoid(b)
    full = jnp.concatenate([hist.astype(u.dtype), u], axis=1)
    y = lax.conv_general_dilated(full, w_dw.astype(u.dtype)[:, None, :], (1,), 'VALID',
                                 dimension_numbers=('NWC', 'WIO', 'NWC'),
                                 feature_group_count=D_MODEL)
    y = jax.nn.silu(rms_norm(y, g_norm))
    return y @ w_out, full[:, -CONV_STATE:]


def conv_mixer(hp, hs, cache_conv, w_in, w_dw, g_norm, w_out):
    zeros = jnp.zeros((hp.shape[0], CONV_STATE, D_MODEL), hp.dtype)
    yp, sp = conv_module(hp, zeros, w_in, w_dw, g_norm, w_out)
    ys, ss = conv_module(hs, cache_conv, w_in, w_dw, g_norm, w_out)
    return yp, ys, (sp, ss.astype(cache_conv.dtype))


def fox_project(h, w_qkv, w_f, b_f, g_q, g_k):
    B, T, _ = h.shape
    q, k, v = jnp.split(h @ w_qkv, 3, axis=-1)
    q = rms_norm(q.reshape(B, T, FOX_HEADS, FOX_HEAD_DIM), g_q)
    k = rms_norm(k.reshape(B, T, FOX_HEADS, FOX_HEAD_DIM), g_k)
    v = v.reshape(B, T, FOX_HEADS, FOX_HEAD_DIM)
    logf = jax.nn.log_sigmoid((h @ w_f + b_f).astype(jnp.float32))
    return q, k, v, logf


def fox_attend(q, c_q, pos_q, k, v, c_k, pos_k):
    s = jnp.einsum('bqhd,bkhd->bhqk', q, k).astype(jnp.float32) * (FOX_HEAD_DIM ** -0.5)
    s = s + jnp.transpose(c_q, (0, 2, 1))[:, :, :, None] - jnp.transpose(c_k, (0, 2, 1))[:, :, None, :]
    s = jnp.where(pos_k[None, :] <= pos_q[:, None], s, -jnp.inf)
    p = jax.nn.softmax(s, axis=-1)
    return jnp.einsum('bhqk,bkhd->bqhd', p.astype(v.dtype), v)


def fox_mixer(hp, hs, cache_k, cache_v, cache_logf, w_qkv, w_f, b_f, g_q, g_k, w_o):
    B, T, _ = hp.shape
    qp, kp, vp, lfp = fox_project(hp, w_qkv, w_f, b_f, g_q, g_k)
    cp = jnp.cumsum(lfp, axis=1)
    pos = jnp.arange(T, dtype=jnp.int32)
    nb = T // Q_BLOCK
    q_blocks = jnp.swapaxes(qp.reshape(B, nb, Q_BLOCK, FOX_HEADS, FOX_HEAD_DIM), 0, 1)
    c_blocks = jnp.swapaxes(cp.reshape(B, nb, Q_BLOCK, FOX_HEADS), 0, 1)
    p_blocks = pos.reshape(nb, Q_BLOCK)
    op = lax.map(lambda a: fox_attend(a[0], a[1], a[2], kp, vp, cp, pos),
                 (q_blocks, c_blocks, p_blocks))
    op = jnp.swapaxes(op, 0, 1).reshape(B, T, D_MODEL)
    Bs, Ts, _ = hs.shape
    P = cache_k.shape[1]
    qs, ks, vs, lfs = fox_project(hs, w_qkv, w_f, b_f, g_q, g_k)
    c_past = jnp.cumsum(cache_logf.astype(jnp.float32), axis=1)
    c_past = c_past - c_past[:, -1:]
    c_new = jnp.cumsum(lfs, axis=1)
    k_all = jnp.concatenate([cache_k.astype(ks.dtype), ks], axis=1)
    v_all = jnp.concatenate([cache_v.astype(vs.dtype), vs], axis=1)
    c_all = jnp.concatenate([c_past, c_new], axis=1)
    pos_q = P + jnp.arange(Ts, dtype=jnp.int32)
    pos_k = jnp.arange(P + Ts, dtype=jnp.int32)
    os_ = fox_attend(qs, c_new, pos_q, k_all, v_all, c_all, pos_k).reshape(Bs, Ts, D_MODEL)
    new = (kp, vp, lfp.astype(hp.dtype),
           ks.astype(cache_k.dtype), vs.astype(cache_v.dtype), lfs.astype(cache_logf.dtype))
    return op @ w_o, os_ @ w_o, new


def gla_project(h, w_qkvr, w_a1, w_a2, b_a):
    B, T, _ = h.shape
    q, k, v, r = jnp.split(h @ w_qkvr, [GLA_QK_WIDTH, 2 * GLA_QK_WIDTH, 2 * GLA_QK_WIDTH + D_MODEL], axis=-1)
    q = q.reshape(B, T, GLA_HEADS, GLA_KEY_DIM).astype(jnp.float32) * (GLA_KEY_DIM ** -0.5)
    k = k.reshape(B, T, GLA_HEADS, GLA_KEY_DIM).astype(jnp.float32)
    v = v.reshape(B, T, GLA_HEADS, GLA_VAL_DIM).astype(jnp.float32)
    loga = jax.nn.log_sigmoid(((h @ w_a1) @ w_a2 + b_a).astype(jnp.float32)) / GLA_TAU
    loga = loga.reshape(B, T, GLA_HEADS, GLA_KEY_DIM)
    return q, k, v, r, loga


def gla_block(S, q, k, v, loga):
    L = q.shape[1]
    b = jnp.cumsum(loga, axis=1)
    causal = jnp.tril(jnp.ones((L, L), dtype=bool))
    diff = b[:, :, None] - b[:, None, :]
    decay = jnp.exp(jnp.where(causal[None, :, :, None, None], diff, -jnp.inf))
    A = jnp.einsum('bthd,bshd,btshd->bhts', q, k, decay)
    o = jnp.einsum('bhts,bshv->bthv', A, v) + jnp.einsum('bthd,bhdv->bthv', q * jnp.exp(b), S)
    b_last = b[:, -1]
    S_new = jnp.exp(b_last)[..., None] * S + jnp.einsum('bshd,bshv->bhdv', k * jnp.exp(b_last[:, None] - b), v)
    return S_new, o


def gla_output(o, r, g_o, w_o, dtype):
    B, T = o.shape[:2]
    o = rms_norm(o.astype(dtype), g_o) * jax.nn.silu(r.reshape(B, T, GLA_HEADS, GLA_VAL_DIM))
    return o.reshape(B, T, D_MODEL) @ w_o


def gla_mixer(hp, hs, state_gla, w_qkvr, w_a1, w_a2, b_a, g_o, w_o):
    B, T, _ = hp.shape
    nc = T // CHUNK
    q, k, v, r, la = gla_project(hp, w_qkvr, w_a1, w_a2, b_a)
    to_blocks = lambda a: jnp.swapaxes(a.reshape(B, nc, CHUNK, *a.shape[2:]), 0, 1)
    S0 = jnp.zeros((B, GLA_HEADS, GLA_KEY_DIM, GLA_VAL_DIM), jnp.float32)
    Sp, o = lax.scan(lambda S, a: gla_block(S, *a), S0,
                     (to_blocks(q), to_blocks(k), to_blocks(v), to_blocks(la)))
    o = jnp.swapaxes(o, 0, 1).reshape(B, T, GLA_HEADS, GLA_VAL_DIM)
    yp = gla_output(o, r, g_o, w_o, hp.dtype)
    qs, ks, vs, rs, las = gla_project(hs, w_qkvr, w_a1, w_a2, b_a)
    Ss, os_ = gla_block(state_gla.astype(jnp.float32), qs, ks, vs, las)
    ys = gla_output(os_, rs, g_o, w_o, hs.dtype)
    return yp, ys, (Sp.astype(hp.dtype), Ss.astype(state_gla.dtype))


def setup_inputs(seed: int = 0) -> dict:
    key = jax.random.key(seed)
    keys = iter(jax.random.split(key, 96))
    f32 = jnp.float32
    nrm = lambda shape, scale: jax.random.normal(next(keys), shape, f32) * scale
    gain = lambda n: 1.0 + 0.05 * jax.random.normal(next(keys), (n,), f32)
    D = D_MODEL
    inp = {}
    inp['x_prompt'] = nrm((BATCH, SEQ, D), 1.0)
    inp['x_sample'] = nrm((DEC_BATCH, DEC_SEQ, D), 1.0)
    inp['cache_conv_l0'] = nrm((DEC_BATCH, CONV_STATE, D), 0.5)
    inp['cache_k_l1'] = nrm((DEC_BATCH, PAST_LEN, FOX_HEADS, FOX_HEAD_DIM), 1.0)
    inp['cache_v_l1'] = nrm((DEC_BATCH, PAST_LEN, FOX_HEADS, FOX_HEAD_DIM), 1.0)
    inp['cache_logf_l1'] = jax.nn.log_sigmoid(3.0 + nrm((DEC_BATCH, PAST_LEN, FOX_HEADS), 1.0))
    inp['state_gla_l2'] = nrm((DEC_BATCH, GLA_HEADS, GLA_KEY_DIM, GLA_VAL_DIM), 2.0)
    inp['cache_conv_l3'] = nrm((DEC_BATCH, CONV_STATE, D), 0.5)

    def ffn(l):
        inp[f'norm_ffn_l{l}'] = gain(D)
        inp[f'ffn_w_up_l{l}'] = nrm((D, D_FF), D ** -0.5)
        inp[f'ffn_w_down_l{l}'] = nrm((D_FF, D), D_FF ** -0.5)

    def conv(l):
        inp[f'norm_mix_l{l}'] = gain(D)
        inp[f'conv_w_in_l{l}'] = nrm((D, 2 * D), D ** -0.5)
        inp[f'conv_w_dw_l{l}'] = nrm((CONV_WIDTH, D), CONV_WIDTH ** -0.5)
        inp[f'conv_norm_l{l}'] = gain(D)
        inp[f'conv_w_out_l{l}'] = nrm((D, D), D ** -0.5)
        ffn(l)

    conv(0)
    inp['norm_mix_l1'] = gain(D)
    inp['fox_w_qkv_l1'] = nrm((D, 3 * D), D ** -0.5)
    inp['fox_w_f_l1'] = nrm((D, FOX_HEADS), D ** -0.5)
    inp['fox_b_f_l1'] = 3.0 + nrm((FOX_HEADS,), 0.5)
    inp['fox_q_norm_l1'] = gain(FOX_HEAD_DIM)
    inp['fox_k_norm_l1'] = gain(FOX_HEAD_DIM)
    inp['fox_w_o_l1'] = nrm((D, D), D ** -0.5)
    ffn(1)
    inp['norm_mix_l2'] = gain(D)
    inp['gla_w_qkvr_l2'] = nrm((D, 2 * GLA_QK_WIDTH + 2 * D), D ** -0.5)
    inp['gla_w_a1_l2'] = nrm((D, GLA_GATE_RANK), D ** -0.5)
    inp['gla_w_a2_l2'] = nrm((GLA_GATE_RANK, GLA_QK_WIDTH), GLA_GATE_RANK ** -0.5)
    inp['gla_b_a_l2'] = nrm((GLA_QK_WIDTH,), 0.1)
    inp['gla_o_norm_l2'] = gain(GLA_VAL_DIM)
    inp['gla_w_o_l2'] = nrm((D, D), D ** -0.5)
    ffn(2)
    conv(3)
    return inp


def reference(x_prompt, x_sample, cache_conv_l0, cache_k_l1, cache_v_l1, cache_logf_l1, state_gla_l2, cache_conv_l3,
              norm_mix_l0, conv_w_in_l0, conv_w_dw_l0, conv_norm_l0, conv_w_out_l0, norm_ffn_l0, ffn_w_up_l0, ffn_w_down_l0,
              norm_mix_l1, fox_w_qkv_l1, fox_w_f_l1, fox_b_f_l1, fox_q_norm_l1, fox_k_norm_l1, fox_w_o_l1, norm_ffn_l1, ffn_w_up_l1, ffn_w_down_l1,
              norm_mix_l2, gla_w_qkvr_l2, gla_w_a1_l2, gla_w_a2_l2, gla_b_a_l2, gla_o_norm_l2, gla_w_o_l2, norm_ffn_l2, ffn_w_up_l2, ffn_w_down_l2,
              norm_mix_l3, conv_w_in_l3, conv_w_dw_l3, conv_norm_l3, conv_w_out_l3, norm_ffn_l3, ffn_w_up_l3, ffn_w_down_l3):
    norm_mix = [norm_mix_l0, norm_mix_l1, norm_mix_l2, norm_mix_l3]
    mix_args = [
        (cache_conv_l0, conv_w_in_l0, conv_w_dw_l0, conv_norm_l0, conv_w_out_l0),
        (cache_k_l1, cache_v_l1, cache_logf_l1, fox_w_qkv_l1, fox_w_f_l1, fox_b_f_l1, fox_q_norm_l1, fox_k_norm_l1, fox_w_o_l1),
        (state_gla_l2, gla_w_qkvr_l2, gla_w_a1_l2, gla_w_a2_l2, gla_b_a_l2, gla_o_norm_l2, gla_w_o_l2),
        (cache_conv_l3, conv_w_in_l3, conv_w_dw_l3, conv_norm_l3, conv_w_out_l3),
    ]
    ffn_args = [(norm_ffn_l0, ffn_w_up_l0, ffn_w_down_l0), (norm_ffn_l1, ffn_w_up_l1, ffn_w_down_l1),
                (norm_ffn_l2, ffn_w_up_l2, ffn_w_down_l2), (norm_ffn_l3, ffn_w_up_l3, ffn_w_down_l3)]
    mixers = [conv_mixer, fox_mixer, gla_mixer]
    xp, xs = x_prompt, x_sample
    new_state = []
    for i in range(DEPTH):
        mixer = mixers[i % N_MIXERS]
        dp, ds, st = mixer(rms_norm(xp, norm_mix[i]), rms_norm(xs, norm_mix[i]), *mix_args[i])
        xp = xp + dp
        xs = xs + ds
        new_state.extend(st)
        g_f, w_up, w_down = ffn_args[i]
        xp = xp + sqrelu_mlp(rms_norm(xp, g_f), w_up, w_down)
        xs = xs + sqrelu_mlp(rms_norm(xs, g_f), w_up, w_down)
    y_prompt, y_sample = xp, xs
    return (y_prompt, y_sample, *new_state)
```

```python
import contextlib
import numpy as np
import concourse.bass as bass
import concourse.mybir as mybir
from concourse.bass_utils import run_bass_kernel_spmd

F32 = mybir.dt.float32
BF16 = mybir.dt.bfloat16
AF = mybir.ActivationFunctionType
ALU = mybir.AluOpType
AX = mybir.AxisListType

FULL = dict(D=2048, T=2048, TS=32, P=2048, DFF=8192, MT=512, MTG=256)
NDS = 40
WELEMS = 4096
CW = 31
CS = 30
EPS = 1e-6


INPUT_NAMES = (
    "x_prompt",
    "x_sample",
    "cache_conv_l0",
    "cache_k_l1",
    "cache_v_l1",
    "cache_logf_l1",
    "state_gla_l2",
    "cache_conv_l3",
    "norm_mix_l0",
    "conv_w_in_l0",
    "conv_w_dw_l0",
    "conv_norm_l0",
    "conv_w_out_l0",
    "norm_ffn_l0",
    "ffn_w_up_l0",
    "ffn_w_down_l0",
    "norm_mix_l1",
    "fox_w_qkv_l1",
    "fox_w_f_l1",
    "fox_b_f_l1",
    "fox_q_norm_l1",
    "fox_k_norm_l1",
    "fox_w_o_l1",
    "norm_ffn_l1",
    "ffn_w_up_l1",
    "ffn_w_down_l1",
    "norm_mix_l2",
    "gla_w_qkvr_l2",
    "gla_w_a1_l2",
    "gla_w_a2_l2",
    "gla_b_a_l2",
    "gla_o_norm_l2",
    "gla_w_o_l2",
    "norm_ffn_l2",
    "ffn_w_up_l2",
    "ffn_w_down_l2",
    "norm_mix_l3",
    "conv_w_in_l3",
    "conv_w_dw_l3",
    "conv_norm_l3",
    "conv_w_out_l3",
    "norm_ffn_l3",
    "ffn_w_up_l3",
    "ffn_w_down_l3",
)


class Tk:
    __slots__ = ("ap", "w", "r")

    def __init__(self, ap):
        self.ap = ap
        self.w = None
        self.r = {}


class Builder:
    def __init__(self, nc, es):
        self.nc = nc
        self.es = es
        self.engs = {"pe": nc.tensor, "act": nc.scalar, "dve": nc.vector, "pool": nc.gpsimd, "sp": nc.sync}
        self.sems = {}
        for k in self.engs:
            self.sems[k] = es.enter_context(nc.semaphore("s_" + k))
        for i in range(NDS):
            self.sems[("d", i)] = es.enter_context(nc.semaphore("d%d" % i))
        self.cnt = {k: 0 for k in self.sems}
        self.waited = {k: {} for k in self.engs}
        self.dma_i = 0
        self.rr = 0

    def wait(self, e, s, v):
        if self.waited[e].get(s, 0) >= v:
            return
        self.engs[e].wait_ge(self.sems[s], v)
        self.waited[e][s] = v

    def _deps(self, e, rd, wr, isdma=False):
        deps = {}

        def add(tok, raw):
            if tok is None:
                return
            s, v = tok
            if s == e and not isdma:
                if e == "pe":
                    return
            if deps.get(s, 0) < v:
                deps[s] = v

        for t in rd:
            add(t.w, True)
        for t in wr:
            add(t.w, False)
            for s, v in t.r.items():
                add((s, v), False)
        for s, v in deps.items():
            assert v <= self.cnt[s], ("wait on a not-yet-emitted signal", e, s, v, self.cnt[s])
            self.wait(e, s, v)

    def _mark(self, tok, rd, wr):
        s, v = tok
        for t in rd:
            if t.r.get(s, 0) < v:
                t.r[s] = v
        for t in wr:
            t.w = tok
            t.r = {}

    def op(self, e, fn, rd=(), wr=(), sig=True):
        self._deps(e, rd, wr)
        ins = fn(self.engs[e])
        if sig:
            self.cnt[e] += 1
            ins.then_inc(self.sems[e], 1)
            tok = (e, self.cnt[e])
        else:
            assert e == "pe"
            tok = (e, self.cnt[e] + 1)
        self._mark(tok, rd, wr)

    def dma(self, out, in_, rd=(), wr=(), q="sp", **kw):
        slot = ("d", self.dma_i % NDS)
        self.dma_i += 1
        self.cnt[slot] += 16
        tgt = self.cnt[slot]
        if tgt > 16:
            self.wait(q, slot, tgt - 16)
        self._deps(q, rd, wr, isdma=True)
        self.engs[q].dma_start(out=out, in_=in_, **kw).then_inc(self.sems[slot], 16)
        self._mark((slot, tgt), rd, wr)

    def barrier(self):
        for e in self.engs:
            for s, v in self.cnt.items():
                if s != e and v > 0:
                    self.wait(e, s, v)

    def sb(self, st, name, shape, dt=F32):
        self.rr += 1
        name = "%s_%d" % (name, self.rr)
        return Tk(st.enter_context(self.nc.sbuf_tensor(name, list(shape), dt)).ap())


def build_program(cfg):
    D, T, TS, P, DFF, MT, MTG = (cfg[k] for k in ("D", "T", "TS", "P", "DFF", "MT", "MTG"))
    KC = D // 128
    FC = DFF // 128
    H = D // 128
    GH = 4
    DK = D // 8
    DV = D // 4
    QK = GH * DK
    NDC = QK // 128
    EPH = DK // 128
    VPH = DV // 128
    R = T + TS
    assert T % MT == 0 and T % MTG == 0 and P % 128 == 0 and TS >= CS and TS <= 128

    nc = bass.Bass("TRN2", target_bir_lowering=False)
    es = contextlib.ExitStack()
    dt_in = lambda name, shape: nc.dram_tensor(name, list(shape), F32, kind="ExternalInput").ap()
    dt_out = lambda name, shape: nc.dram_tensor(name, list(shape), F32, kind="ExternalOutput").ap()
    I = {}
    I["x_prompt"] = dt_in("x_prompt", [T, D])
    I["x_sample"] = dt_in("x_sample", [TS, D])
    I["cache_conv_l0"] = dt_in("cache_conv_l0", [CS, D])
    I["cache_k_l1"] = dt_in("cache_k_l1", [P, D])
    I["cache_v_l1"] = dt_in("cache_v_l1", [P, D])
    I["cache_logf_l1"] = dt_in("cache_logf_l1", [P, H])
    I["state_gla_l2"] = dt_in("state_gla_l2", [QK, DV])
    I["cache_conv_l3"] = dt_in("cache_conv_l3", [CS, D])
    for l in range(4):
        I["norm_mix_l%d" % l] = dt_in("norm_mix_l%d" % l, [D])
        I["norm_ffn_l%d" % l] = dt_in("norm_ffn_l%d" % l, [D])
        I["ffn_w_up_l%d" % l] = dt_in("ffn_w_up_l%d" % l, [D, DFF])
        I["ffn_w_down_l%d" % l] = dt_in("ffn_w_down_l%d" % l, [DFF, D])
    for l in (0, 3):
        I["conv_w_in_l%d" % l] = dt_in("conv_w_in_l%d" % l, [D, 2 * D])
        I["conv_w_dw_l%d" % l] = dt_in("conv_w_dw_l%d" % l, [CW, D])
        I["conv_norm_l%d" % l] = dt_in("conv_norm_l%d" % l, [D])
        I["conv_w_out_l%d" % l] = dt_in("conv_w_out_l%d" % l, [D, D])
    I["fox_w_qkv_l1"] = dt_in("fox_w_qkv_l1", [D, 3 * D])
    I["fox_w_f_l1"] = dt_in("fox_w_f_l1", [D, H])
    I["fox_b_f_l1"] = dt_in("fox_b_f_l1", [H])
    I["fox_q_norm_l1"] = dt_in("fox_q_norm_l1", [128])
    I["fox_k_norm_l1"] = dt_in("fox_k_norm_l1", [128])
    I["fox_w_o_l1"] = dt_in("fox_w_o_l1", [D, D])
    I["gla_w_qkvr_l2"] = dt_in("gla_w_qkvr_l2", [D, 2 * QK + 2 * D])
    I["gla_w_a1_l2"] = dt_in("gla_w_a1_l2", [D, 16])
    I["gla_w_a2_l2"] = dt_in("gla_w_a2_l2", [16, QK])
    I["gla_b_a_l2"] = dt_in("gla_b_a_l2", [QK])
    I["gla_o_norm_l2"] = dt_in("gla_o_norm_l2", [DV])
    I["gla_w_o_l2"] = dt_in("gla_w_o_l2", [D, D])
    assert set(I) == set(INPUT_NAMES), set(I) ^ set(INPUT_NAMES)
    O = {}
    O["y_prompt"] = dt_out("y_prompt", [T, D])
    O["y_sample"] = dt_out("y_sample", [TS, D])
    O["conv0_p"] = dt_out("conv0_p", [CS, D])
    O["conv0_s"] = dt_out("conv0_s", [CS, D])
    O["k_p"] = dt_out("k_p", [T, D])
    O["v_p"] = dt_out("v_p", [T, D])
    O["lf_p"] = dt_out("lf_p", [T, H])
    O["k_s"] = dt_out("k_s", [TS, D])
    O["v_s"] = dt_out("v_s", [TS, D])
    O["lf_s"] = dt_out("lf_s", [TS, H])
    O["gla_p"] = dt_out("gla_p", [QK, DV])
    O["gla_s"] = dt_out("gla_s", [QK, DV])
    O["conv3_p"] = dt_out("conv3_p", [CS, D])
    O["conv3_s"] = dt_out("conv3_s", [CS, D])
    XA = nc.dram_tensor("XA", [R, D], F32).ap()
    XB = nc.dram_tensor("XB", [R, D], F32).ap()
    QT = nc.dram_tensor("QT", [D, R], BF16).ap()
    KT = nc.dram_tensor("KT", [D, R], BF16).ap()
    VV = nc.dram_tensor("VV", [R, D], BF16).ap()
    OT = nc.dram_tensor("OT", [D, R], BF16).ap()

    b = Builder(nc, es)
    op, dma = b.op, b.dma
    gst = es

    ident_bf = b.sb(gst, "ident_bf", [128, 128], BF16)
    ident_f = b.sb(gst, "ident_f", [128, 128], F32)
    ones_f = b.sb(gst, "ones_f", [128, 128], F32)
    ones_bf = b.sb(gst, "ones_bf", [128, 128], BF16)
    tri_f = b.sb(gst, "tri_f", [128, 128], F32)
    tri_bf = b.sb(gst, "tri_bf", [128, 128], BF16)
    for idt in (ident_bf, ident_f):
        op("pool", lambda e, t=idt: e.memset(t.ap, 0.0), wr=[idt])
        op("pool", lambda e, t=idt: e.affine_select(out=t.ap, in_=t.ap, compare_op=ALU.not_equal, fill=1.0, base=0,
                                                     pattern=[[-1, 128]], channel_multiplier=1), rd=[idt], wr=[idt])
    for o_ in (ones_f, ones_bf):
        op("pool", lambda e, t=o_: e.memset(t.ap, 1.0), wr=[o_])
    for tr in (tri_f, tri_bf):
        op("pool", lambda e, t=tr: e.memset(t.ap, 1.0), wr=[tr])
        op("pool", lambda e, t=tr: e.affine_select(out=t.ap, in_=t.ap, compare_op=ALU.is_ge, fill=0.0, base=0,
                                                    pattern=[[1, 128]], channel_multiplier=-1), rd=[tr], wr=[tr])
    banks = [Tk(es.enter_context(nc.psum_tensor("bank%d" % i, [128, 512], F32)).ap()) for i in range(8)]
    bank_i = [0]

    def bank():
        t = banks[bank_i[0] % 7]
        bank_i[0] += 1
        assert t.w is None or t.r, "PSUM bank re-allocated before its last result was read"
        return t

    def bfv(t):
        return t.ap.bitcast(BF16)

    NWS, NWB = 2, 3
    wst = [b.sb(gst, "wst%d" % i, [128, WELEMS], F32) for i in range(NWS)]
    wbf = [b.sb(gst, "wbf%d" % i, [128, WELEMS], BF16) for i in range(NWB)]
    wi = [0, 0]
    cast_rr = [0]
    CAST_ENGS = ["dve", "act"]
    wscr = {}
    WSCR_BLKS = cfg.get("wscr_blks", 32)

    def cast(dst_ap, src_ap, rd, wr, eng=None):
        if eng is None:
            eng = CAST_ENGS[cast_rr[0] % len(CAST_ENGS)]
            cast_rr[0] += 1
        if eng == "act":
            op("act", lambda e: e.copy(out=dst_ap, in_=src_ap), rd=rd, wr=wr)
        else:
            op(eng, lambda e: e.tensor_copy(out=dst_ap, in_=src_ap), rd=rd, wr=wr)

    def load_w(W, r0, nr, c0, ncol):
        G = nr // 128
        n = G * ncol
        assert n <= WELEMS
        wname = W.tensor.name
        if wname not in wscr:
            wscr[wname] = (nc.dram_tensor(wname + "_bfs", [WSCR_BLKS, 128, WELEMS], BF16).ap(), {})
        scr, blocks = wscr[wname]
        key = (r0, nr, c0, ncol)
        j = wi[1] % len(wbf)
        wi[1] += 1
        d_ap = wbf[j].ap[:, 0:n].rearrange("p (g n) -> p g n", g=G)
        if key in blocks:
            idx, btk = blocks[key]
            dma(wbf[j].ap[:, 0:n], scr[idx, :, 0:n], rd=[btk], wr=[wbf[j]])
            return wbf[j], d_ap
        i = wi[0] % NWS
        wi[0] += 1
        s_ap = wst[i].ap[:, 0:n].rearrange("p (g n) -> p g n", g=G)
        dma(s_ap, W[r0:r0 + nr, c0:c0 + ncol].rearrange("(g p) n -> p g n", p=128), wr=[wst[i]])
        cast(wbf[j].ap[:, 0:n], wst[i].ap[:, 0:n], rd=[wst[i]], wr=[wbf[j]])
        idx = len(blocks)
        assert idx < WSCR_BLKS, (wname, idx)
        btk = Tk(None)
        blocks[key] = (idx, btk)
        dma(scr[idx, :, 0:n], wbf[j].ap[:, 0:n], rd=[wbf[j]], wr=[btk], q="act")
        return wbf[j], d_ap

    @contextlib.contextmanager
    def extra_wbufs(st, n):
        for i in range(n):
            wbf.append(b.sb(st, "wbfx%d" % i, [128, WELEMS], BF16))
        try:
            yield
        finally:
            del wbf[NWB:]

    def subs_of(nt):
        return [(s, min(128, nt - 128 * s)) for s in range((nt + 127) // 128)]

    def load_gT(st, name, vec, n):
        t = b.sb(st, name, [128, n], F32)
        with nc.allow_non_contiguous_dma(reason="tiny gain vector"):
            dma(t.ap, vec.rearrange("(c p) -> p c", p=128), wr=[t])
        return t

    def load_bc(st, name, vec, n, reps=1):
        t = b.sb(st, name, [128, reps * n], F32)
        for r_ in range(reps):
            dma(t.ap[:, r_ * n:(r_ + 1) * n], vec.partition_broadcast(128), wr=[t])
        return t

    def load_x(xt, src, nt):
        for s, rows in subs_of(nt):
            dma(xt.ap[:rows, s, :], src[s * 128:s * 128 + rows, :], wr=[xt])

    def store_x(xt, dst, nt):
        for s, rows in subs_of(nt):
            dma(dst[s * 128:s * 128 + rows, :], xt.ap[:rows, s, :], rd=[xt], q="act")

    def rstd_from_ss(ss_ap, rd_t, tmp, out, n_inv, shape_sl):
        op("act", lambda e: e.activation(out=tmp.ap[shape_sl], in_=ss_ap, func=AF.Sqrt, scale=n_inv, bias=EPS_T.ap[shape_sl[0], 0:1]),
           rd=[rd_t, EPS_T], wr=[tmp])
        op("dve", lambda e: e.reciprocal(out=out.ap[shape_sl], in_=tmp.ap[shape_sl]), rd=[tmp], wr=[out])

    EPS_T = b.sb(gst, "eps_t", [128, 1], F32)
    op("pool", lambda e: e.memset(EPS_T.ap, EPS), wr=[EPS_T])
    junk = b.sb(gst, "junk", [128, D], BF16)
    hn = [b.sb(gst, "hn%d" % i, [128, D], BF16) for i in range(1)]
    nstat = b.sb(gst, "nstat", [128, 16], F32)
    nstat2 = b.sb(gst, "nstat2", [128, 16], F32)
    nstat3 = b.sb(gst, "nstat3", [128, 16], F32)
    hn_i = [0]

    def norm_T(xt, nt, gT, hT):
        for s, rows in subs_of(nt):
            op("act", lambda e: e.activation(out=junk.ap[:rows, :], in_=xt.ap[:rows, s, :], func=AF.Square,
                                             accum_out=nstat.ap[:rows, s:s + 1]), rd=[xt], wr=[junk, nstat])
            op("act", lambda e: e.activation(out=nstat2.ap[:rows, s:s + 1], in_=nstat.ap[:rows, s:s + 1], func=AF.Sqrt,
                                             scale=1.0 / D, bias=EPS_T.ap[:rows, 0:1]), rd=[nstat, EPS_T], wr=[nstat2])
            op("dve", lambda e: e.reciprocal(out=nstat3.ap[:rows, s:s + 1], in_=nstat2.ap[:rows, s:s + 1]), rd=[nstat2], wr=[nstat3])
            h_ = hn[0]
            hn_i[0] += 1
            op("act", lambda e: e.activation(out=h_.ap[:rows, :], in_=xt.ap[:rows, s, :], func=AF.Copy,
                                             scale=nstat3.ap[:rows, s:s + 1]), rd=[xt, nstat3], wr=[h_])
            for c0 in range(0, KC, 4):
                bk = bank()
                n4 = min(4, KC - c0)
                for j in range(n4):
                    c = c0 + j
                    op("pe", lambda e: e.transpose(out=bfv(bk)[:, j * 128:j * 128 + rows], in_=h_.ap[:rows, c * 128:(c + 1) * 128],
                                                   identity=ident_bf.ap[:rows, :rows]), rd=[h_, ident_bf], wr=[bk], sig=(j == n4 - 1))
                for j in range(n4):
                    c = c0 + j
                    eng = "dve" if j % 2 == 0 else "pool_no"
                    op("dve", lambda e: e.tensor_scalar(out=hT.ap[:, c, s * 128:s * 128 + rows], in0=bfv(bk)[:, j * 128:j * 128 + rows],
                                                        scalar1=gT.ap[:, c:c + 1], scalar2=None, op0=ALU.mult), rd=[bk, gT], wr=[hT])

    def proj_fm(hT, nt, W, blocks, cb):
        kin = W.shape[0] // 128
        for blk in blocks:
            wt, wap = load_w(W, 0, W.shape[0], blk, 256)
            for half in range(2):
                bk = bank()
                for kc in range(kin):
                    op("pe", lambda e: e.matmul(bk.ap[:, :nt], lhsT=wap[:, kc, half * 128:(half + 1) * 128], rhs=hT.ap[:, kc, :nt],
                                                start=(kc == 0), stop=(kc == kin - 1)), rd=[wt, hT], wr=[bk], sig=(kc == kin - 1))
                cb(blk + half * 128, bk)

    def proj_tm(aT, nt, W, r0, kin, c0, ncols, cb, hook=None, cbb=None):
        subs = subs_of(nt)
        for cc in range(0, ncols, 512):
            w = min(512, ncols - cc)
            bks = [bank() for _ in subs]
            G = max(1, min(kin, WELEMS // w))
            for kg in range(0, kin, G):
                g_n = min(G, kin - kg)
                wt, wap = load_w(W, r0 + kg * 128, g_n * 128, c0 + cc, w)
                for g in range(g_n):
                    k = kg + g
                    for s, rows in subs:
                        op("pe", lambda e: e.matmul(bks[s].ap[:rows, :w], lhsT=aT.ap[:, k, s * 128:s * 128 + rows], rhs=wap[:, g, :w],
                                                    start=(k == 0), stop=(k == kin - 1)), rd=[wt, aT], wr=[bks[s]],
                           sig=(k == kin - 1 or (g == g_n - 1 and s == subs[-1][0])))
            if cbb is not None:
                cbb(cc, w, [(s, rows, bks[s]) for s, rows in subs])
            else:
                for s, rows in subs:
                    cb(s, rows, cc, w, bks[s])
            if hook is not None:
                hook()

    def proj_tm_add(aT, nt, W, kin, xt):
        def cb(s, rows, cc, w, bk):
            op("dve", lambda e: e.tensor_tensor(out=xt.ap[:rows, s, cc:cc + w], in0=bk.ap[:rows, :w], in1=xt.ap[:rows, s, cc:cc + w],
                                                op=ALU.add), rd=[bk, xt], wr=[xt])
        proj_tm(aT, nt, W, 0, kin, 0, D, cb)

    def transpose_f32_rows(src, n, dst, dst_sl):
        for c in range(KC):
            bk = bank()
            op("pe", lambda e: e.transpose(out=bk.ap[:, :n], in_=src.ap[:n, c * 128:(c + 1) * 128], identity=ident_f.ap[:n, :n]),
               rd=[src, ident_f], wr=[bk])
            op("dve", lambda e: e.tensor_copy(out=dst.ap[:, c, dst_sl], in_=bk.ap[:, :n]), rd=[bk], wr=[dst])

    def tiles_of(mt):
        lst = [("p", r0, mt) for r0 in range(0, T, mt)]
        lst.append(("s", T, TS))
        return lst

    def xsrc(l, grp, r0, nt):
        if l == 0:
            return I["x_prompt"][r0:r0 + nt, :] if grp == "p" else I["x_sample"][0:nt, :]
        return XB[r0:r0 + nt, :]

    def ydst(l, grp, r0, nt):
        if l == 3:
            return O["y_prompt"][r0:r0 + nt, :] if grp == "p" else O["y_sample"][0:nt, :]
        return XB[r0:r0 + nt, :]

    def ffn_phase(l):
        with contextlib.ExitStack() as st:
            NS = (MT + 127) // 128
            xt = b.sb(st, "f_xt", [128, NS, D], F32)
            hT = b.sb(st, "f_hT", [128, KC, MT], BF16)
            aT = b.sb(st, "f_aT", [128, FC, MT], BF16)
            sq = [b.sb(st, "f_sq%d" % i, [128, MT], F32) for i in range(2)]
            gT = load_gT(st, "f_gT", I["norm_ffn_l%d" % l], KC)
            Wu, Wd = I["ffn_w_up_l%d" % l], I["ffn_w_down_l%d" % l]
            st.enter_context(extra_wbufs(st, cfg.get("xw_ffn", 3)))
            k_ = [0]
            for grp, r0, nt in tiles_of(MT):
                load_x(xt, XA[r0:r0 + nt, :], nt)
                norm_T(xt, nt, gT, hT)

                def cb(col0, bk):
                    fc = col0 // 128
                    s_ = sq[k_[0] % 2]
                    k_[0] += 1
                    op("act", lambda e: e.activation(out=s_.ap[:, :nt], in_=bk.ap[:, :nt], func=AF.Square), rd=[bk], wr=[s_])
                    op("dve", lambda e: e.scalar_tensor_tensor(out=aT.ap[:, fc, :nt], in0=bk.ap[:, :nt], scalar=0.0, in1=s_.ap[:, :nt],
                                                               op0=ALU.is_gt, op1=ALU.mult), rd=[bk, s_], wr=[aT])
                proj_fm(hT, nt, Wu, list(range(0, DFF, 256)), cb)
                proj_tm_add(aT, nt, Wd, FC, xt)
                store_x(xt, ydst(l, grp, r0, nt), nt)
            b.barrier()

    def conv_phase(l):
        with contextlib.ExitStack() as st:
            NS = (MT + 127) // 128
            xt = b.sb(st, "c_xt", [128, NS, D], F32)
            hT = b.sb(st, "c_hT", [128, KC, MT], BF16)
            ub = b.sb(st, "c_ub", [128, KC, CS + MT], BF16)
            ul = b.sb(st, "c_ul", [128, KC, CS], F32)
            a_sb = b.sb(st, "c_a", [128, 2, MT], F32)
            sg = [b.sb(st, "c_sg%d" % i, [128, MT], F32) for i in range(2)]
            ycp = b.sb(st, "c_y", [128, KC, MT], F32)
            zT = b.sb(st, "c_zT", [128, KC, MT], BF16)
            rsb = b.sb(st, "c_rsb", [128, MT], F32)
            rsb2 = b.sb(st, "c_rsb2", [128, MT], F32)
            dg = [b.sb(st, "c_dg%d" % i, [128, 128], BF16) for i in range(16)]
            wdw_tm = b.sb(st, "c_wdwtm", [32, D], F32)
            wdwT = b.sb(st, "c_wdwT", [128, KC, CW], F32)
            hist_tm = wdw_tm
            st_tm = wdw_tm
            gT = load_gT(st, "c_gT", I["norm_mix_l%d" % l], KC)
            gcT = load_gT(st, "c_gcT", I["conv_norm_l%d" % l], KC)
            Win, Wout = I["conv_w_in_l%d" % l], I["conv_w_out_l%d" % l]
            dma(wdw_tm.ap[:CW, :], I["conv_w_dw_l%d" % l], wr=[wdw_tm])
            transpose_f32_rows(wdw_tm, CW, wdwT, slice(0, CW))
            tiles = tiles_of(MT)
            k_ = [0]
            for ti, (grp, r0, nt) in enumerate(tiles):
                last = (ti + 1 == len(tiles)) or tiles[ti + 1][0] != grp
                first = (ti == 0) or tiles[ti - 1][0] != grp
                load_x(xt, xsrc(l, grp, r0, nt), nt)
                norm_T(xt, nt, gT, hT)
                if grp == "p" and first:
                    op("pool", lambda e: e.memset(ub.ap[:, :, 0:CS], 0.0), wr=[ub])
                elif grp == "p":
                    op("dve", lambda e: e.tensor_copy(out=ub.ap[:, :, 0:CS], in_=ub.ap[:, :, pnt:pnt + CS]), rd=[ub], wr=[ub])
                else:
                    dma(hist_tm.ap[:CS, :], I["cache_conv_l%d" % l], wr=[hist_tm])
                    transpose_f32_rows(hist_tm, CS, ub, slice(0, CS))
                pnt = nt

                for cb2 in range(0, D, 256):
                    def cb_a(col0, bk):
                        j = (col0 - cb2) // 128
                        op("act", lambda e: e.copy(out=a_sb.ap[:, j, :nt], in_=bk.ap[:, :nt]), rd=[bk], wr=[a_sb])
                    proj_fm(hT, nt, Win, [cb2], cb_a)

                    def cb_b(col0, bk):
                        j = (col0 - D - cb2) // 128
                        c = (col0 - D) // 128
                        s_ = sg[k_[0] % 2]
                        k_[0] += 1
                        op("act", lambda e: e.activation(out=s_.ap[:, :nt], in_=bk.ap[:, :nt], func=AF.Sigmoid), rd=[bk], wr=[s_])
                        op("dve", lambda e: e.tensor_tensor(out=ub.ap[:, c, CS:CS + nt], in0=a_sb.ap[:, j, :nt], in1=s_.ap[:, :nt], op=ALU.mult),
                           rd=[a_sb, s_], wr=[ub])
                        if last:
                            op("dve", lambda e: e.tensor_tensor(out=ul.ap[:, c, :], in0=a_sb.ap[:, j, nt - CS:nt], in1=s_.ap[:, nt - CS:nt],
                                                                op=ALU.mult), rd=[a_sb, s_], wr=[ul])
                    proj_fm(hT, nt, Win, [D + cb2], cb_b)
                if last:
                    for c in range(KC):
                        bk = bank()
                        op("pe", lambda e: e.transpose(out=bk.ap[:CS, :128], in_=ul.ap[:, c, :], identity=ident_f.ap), rd=[ul, ident_f], wr=[bk])
                        op("dve", lambda e: e.tensor_copy(out=st_tm.ap[:CS, c * 128:(c + 1) * 128], in_=bk.ap[:CS, :128]), rd=[bk], wr=[st_tm])
                    dma(O["conv%d_%s" % (l, grp)], st_tm.ap[:CS, :], rd=[st_tm], q="act")
                ssb = banks[7]
                for c in range(KC):
                    bk = bank()
                    for j in range(CW):
                        d_ = dg[k_[0] % 16]
                        k_[0] += 1
                        if k_[0] % 2 == 0:
                            op("dve", lambda e: e.tensor_scalar(out=d_.ap, in0=ident_bf.ap, scalar1=wdwT.ap[:, c, j:j + 1], scalar2=None, op0=ALU.mult),
                               rd=[ident_bf, wdwT], wr=[d_])
                        else:
                            op("act", lambda e: e.activation(out=d_.ap, in_=ident_bf.ap, func=AF.Copy, scale=wdwT.ap[:, c, j:j + 1]),
                               rd=[ident_bf, wdwT], wr=[d_])
                        op("pe", lambda e: e.matmul(bk.ap[:, :nt], lhsT=d_.ap, rhs=ub.ap[:, c, j:j + nt], start=(j == 0), stop=(j == CW - 1)),
                           rd=[d_, ub], wr=[bk])
                    op("act", lambda e: e.copy(out=ycp.ap[:, c, :nt], in_=bk.ap[:, :nt]), rd=[bk], wr=[ycp])
                    s_ = sg[k_[0] % 2]
                    k_[0] += 1
                    op("act", lambda e: e.activation(out=s_.ap[:, :nt], in_=bk.ap[:, :nt], func=AF.Square), rd=[bk], wr=[s_])
                    op("pe", lambda e: e.matmul(ssb.ap[:, :nt], lhsT=ones_f.ap, rhs=s_.ap[:, :nt], start=(c == 0), stop=(c == KC - 1)),
                       rd=[ones_f, s_], wr=[ssb])
                op("act", lambda e: e.activation(out=rsb2.ap[:, :nt], in_=ssb.ap[:, :nt], func=AF.Sqrt, scale=1.0 / D, bias=EPS_T.ap[:, 0:1]),
                   rd=[ssb, EPS_T], wr=[rsb2])
                op("dve", lambda e: e.reciprocal(out=rsb.ap[:, :nt], in_=rsb2.ap[:, :nt]), rd=[rsb2], wr=[rsb])
                for c in range(KC):
                    s_ = sg[k_[0] % 2]
                    k_[0] += 1
                    op("dve", lambda e: e.scalar_tensor_tensor(out=s_.ap[:, :nt], in0=ycp.ap[:, c, :nt], scalar=gcT.ap[:, c:c + 1], in1=rsb.ap[:, :nt],
                                                               op0=ALU.mult, op1=ALU.mult), rd=[ycp, gcT, rsb], wr=[s_])
                    op("act", lambda e: e.activation(out=zT.ap[:, c, :nt], in_=s_.ap[:, :nt], func=AF.Silu), rd=[s_], wr=[zT])
                proj_tm_add(zT, nt, Wout, KC, xt)
                store_x(xt, XA[r0:r0 + nt, :], nt)
            b.barrier()

    def fox_phase(l):
        NCP = T // 128
        NCS = P // 128 + 1
        with contextlib.ExitStack() as st0:
            c_all = b.sb(st0, "x_call", [128, NCP + NCS, H], F32)
            cref = b.sb(st0, "x_cref", [128, NCP + NCS, H], F32)
            with contextlib.ExitStack() as st:
                NS = (MT + 127) // 128
                xt = b.sb(st, "x_xt", [128, NS, D], F32)
                hT = b.sb(st, "x_hT", [128, KC, MT], BF16)
                qTm = b.sb(st, "x_qTm", [128, H, MT], BF16)
                kTm = b.sb(st, "x_kTm", [128, H, MT], BF16)
                sqt = [b.sb(st, "x_sqt%d" % i, [128, 512], F32) for i in range(4)]
                t1 = [b.sb(st, "x_t1%d" % i, [128, 512], F32) for i in range(4)]
                stg = [b.sb(st, "x_stg%d" % i, [128, 512], F32) for i in range(4)]
                nb = [b.sb(st, "x_nb%d" % i, [128, 512], BF16) for i in range(8)]
                defer = []
                defer_old = []

                def flush_defer():
                    while defer_old:
                        defer_old.pop(0)()
                    defer_old.extend(defer)
                    del defer[:]
                s4 = [b.sb(st, "x_s4%d" % i, [128, 12], F32) for i in range(4)]
                lfall = b.sb(st, "x_lfall", [128, NCP + NCS, H], F32)
                zt = b.sb(st, "x_zt", [128, 2 * H], F32)
                wf_st = b.sb(st, "x_wfst", [128, KC, H], F32)
                wfb = b.sb(st, "x_wfb", [128, KC, H], BF16)
                bfb = load_bc(st, "x_bfb", I["fox_b_f_l1"], H)
                gqb = load_bc(st, "x_gqb", I["fox_q_norm_l1"], 128, reps=4)
                gkb = load_bc(st, "x_gkb", I["fox_k_norm_l1"], 128, reps=4)
                gT = load_gT(st, "x_gT", I["norm_mix_l%d" % l], KC)
                carry = b.sb(st, "x_carry", [128, H], F32)
                Wqkv = I["fox_w_qkv_l1"]
                dma(wf_st.ap, I["fox_w_f_l1"].rearrange("(c p) h -> p c h", p=128), wr=[wf_st])
                op("dve", lambda e: e.tensor_copy(out=wfb.ap, in_=wf_st.ap), rd=[wf_st], wr=[wfb])
                k_ = [0]

                def cumsum_chunk(ci, rows, first):
                    bk = bank()
                    op("pe", lambda e: e.matmul(bk.ap[:rows, 0:H], lhsT=tri_f.ap[:rows, :rows], rhs=lfall.ap[:rows, ci, :], start=True, stop=True),
                       rd=[tri_f, lfall], wr=[bk])
                    bk2 = bank()
                    op("pe", lambda e: e.matmul(bk2.ap[:, 0:H], lhsT=ones_f.ap[:rows, :], rhs=lfall.ap[:rows, ci, :], start=True, stop=True),
                       rd=[ones_f, lfall], wr=[bk2])
                    if first:
                        op("dve", lambda e: e.tensor_copy(out=c_all.ap[:rows, ci, :], in_=bk.ap[:rows, 0:H]), rd=[bk], wr=[c_all])
                        op("dve", lambda e: e.tensor_copy(out=carry.ap, in_=bk2.ap[:, 0:H]), rd=[bk2], wr=[carry])
                    else:
                        op("dve", lambda e: e.tensor_tensor(out=c_all.ap[:rows, ci, :], in0=bk.ap[:rows, 0:H], in1=carry.ap[:rows, :], op=ALU.add),
                           rd=[bk, carry], wr=[c_all])
                        op("dve", lambda e: e.tensor_tensor(out=carry.ap, in0=bk2.ap[:, 0:H], in1=carry.ap, op=ALU.add), rd=[bk2, carry], wr=[carry])
                    op("dve", lambda e: e.tensor_copy(out=cref.ap[:, ci, :], in_=carry.ap), rd=[carry], wr=[cref])

                for grp, r0, nt in tiles_of(MT):
                    load_x(xt, XB[r0:r0 + nt, :], nt)
                    norm_T(xt, nt, gT, hT)
                    kout = O["k_p"] if grp == "p" else O["k_s"]
                    vout = O["v_p"] if grp == "p" else O["v_s"]
                    lout = O["lf_p"] if grp == "p" else O["lf_s"]
                    ro = r0 if grp == "p" else 0

                    def cb_qk(which):
                        gb_ = gqb if which == "q" else gkb
                        dstT = qTm if which == "q" else kTm

                        def cb(cc, w, items):
                            nh = w // 128
                            idx = []
                            for (s, rows, bk) in items:
                                idx.append((k_[0] % 4, k_[0] % 8))
                                k_[0] += 1
                            for (s, rows, bk), (i4, i3) in zip(items, idx):
                                op("act", lambda e: e.activation(out=sqt[i4].ap[:rows, :w], in_=bk.ap[:rows, :w], func=AF.Square), rd=[bk], wr=[sqt[i4]])
                            for (s, rows, bk), (i4, i3) in zip(items, idx):
                                op("dve", lambda e: e.tensor_tensor(out=t1[i4].ap[:rows, :w], in0=bk.ap[:rows, :w], in1=gb_.ap[:rows, :w], op=ALU.mult),
                                   rd=[bk, gb_, sqt[i4]], wr=[t1[i4]])
                            for (s, rows, bk), (i4, i3) in zip(items, idx):
                                op("dve", lambda e: e.tensor_reduce(out=s4[i4].ap[:rows, 0:nh], in_=sqt[i4].ap[:rows, :w].rearrange("p (h d) -> p h d", h=nh),
                                                                    axis=AX.X, op=ALU.add), rd=[sqt[i4]], wr=[s4[i4]])
                            for (s, rows, bk), (i4, i3) in zip(items, idx):
                                op("act", lambda e: e.activation(out=s4[i4].ap[:rows, 4:4 + nh], in_=s4[i4].ap[:rows, 0:nh], func=AF.Sqrt, scale=1.0 / 128,
                                                                 bias=EPS_T.ap[:rows, 0:1]), rd=[s4[i4], EPS_T], wr=[s4[i4]])
                            for (s, rows, bk), (i4, i3) in zip(items, idx):
                                op("dve", lambda e: e.reciprocal(out=s4[i4].ap[:rows, 8:8 + nh], in_=s4[i4].ap[:rows, 4:4 + nh]), rd=[s4[i4]], wr=[s4[i4]])
                            for (s, rows, bk), (i4, i3) in zip(items, idx):
                                for hh in range(nh):
                                    sl = slice(hh * 128, (hh + 1) * 128)
                                    dst_ = stg[i4] if which == "k" else nb[i3]
                                    op("dve", lambda e: e.tensor_scalar(out=dst_.ap[:rows, sl], in0=t1[i4].ap[:rows, sl],
                                                                        scalar1=s4[i4].ap[:rows, 8 + hh:9 + hh], scalar2=None, op0=ALU.mult),
                                       rd=[t1[i4], s4[i4]], wr=[dst_])
                            for (s, rows, bk), (i4, i3) in zip(items, idx):
                                if which == "k":
                                    dma(kout[ro + s * 128:ro + s * 128 + rows, cc:cc + w], stg[i4].ap[:rows, :w], rd=[stg[i4]], q="act")
                                    op("act", lambda e: e.copy(out=nb[i3].ap[:rows, :w], in_=stg[i4].ap[:rows, :w]), rd=[stg[i4]], wr=[nb[i3]])

                                def part_b(i3=i3, rows=rows, s=s, cc=cc, nh=nh, dstT=dstT):
                                    tb = bank()
                                    for hh in range(nh):
                                        op("pe", lambda e: e.transpose(out=bfv(tb)[:, hh * 128:hh * 128 + rows], in_=nb[i3].ap[:rows, hh * 128:(hh + 1) * 128],
                                                                       identity=ident_bf.ap[:rows, :rows]), rd=[nb[i3], ident_bf], wr=[tb], sig=(hh == nh - 1))
                                    h0 = cc // 128
                                    op("act", lambda e: e.copy(out=dstT.ap[:, h0:h0 + nh, s * 128:s * 128 + rows],
                                                               in_=bfv(tb)[:, 0:nh * 128].rearrange("p (h t) -> p h t", h=nh)[:, :, 0:rows]), rd=[tb], wr=[dstT])
                                defer.append(part_b)
                        return cb

                    def cb_v(s, rows, cc, w, bk):
                        i3 = k_[0] % 8
                        i4 = k_[0] % 4
                        k_[0] += 1
                        op("act", lambda e: e.copy(out=stg[i4].ap[:rows, :w], in_=bk.ap[:rows, :w]), rd=[bk], wr=[stg[i4]])
                        op("dve", lambda e: e.tensor_copy(out=nb[i3].ap[:rows, :w], in_=stg[i4].ap[:rows, :w]), rd=[stg[i4]], wr=[nb[i3]])
                        dma(vout[ro + s * 128:ro + s * 128 + rows, cc:cc + w], stg[i4].ap[:rows, :w], rd=[stg[i4]], q="act")
                        dma(VV[r0 + s * 128:r0 + s * 128 + rows, cc:cc + w], nb[i3].ap[:rows, :w], rd=[nb[i3]], q="act")

                    proj_tm(hT, nt, Wqkv, 0, KC, 0, D, None, hook=flush_defer, cbb=cb_qk("q"))
                    proj_tm(hT, nt, Wqkv, 0, KC, D, D, None, hook=flush_defer, cbb=cb_qk("k"))
                    proj_tm(hT, nt, Wqkv, 0, KC, 2 * D, D, cb_v, hook=flush_defer)
                    flush_defer()
                    flush_defer()
                    dma(QT[:, r0:r0 + nt].rearrange("(h p) t -> p h t", p=128), qTm.ap[:, :, :nt], rd=[qTm], q="act")
                    dma(KT[:, r0:r0 + nt].rearrange("(h p) t -> p h t", p=128), kTm.ap[:, :, :nt], rd=[kTm], q="act")
                    if grp == "s":
                        dma(lfall.ap[:, NCP:NCP + P // 128, :], I["cache_logf_l1"].rearrange("(c p) h -> p c h", p=128), wr=[lfall])
                        for ci in range(P // 128):
                            cumsum_chunk(NCP + ci, 128, ci == 0)
                    for s, rows in subs_of(nt):
                        bk = bank()
                        for kc in range(KC):
                            op("pe", lambda e: e.matmul(bk.ap[:rows, 0:H], lhsT=hT.ap[:, kc, s * 128:s * 128 + rows], rhs=wfb.ap[:, kc, :],
                                                        start=(kc == 0), stop=(kc == KC - 1)), rd=[hT, wfb], wr=[bk], sig=(kc == KC - 1))
                        ci = (r0 // 128 + s) if grp == "p" else (NCP + P // 128)
                        op("dve", lambda e: e.tensor_tensor(out=zt.ap[:rows, 0:H], in0=bk.ap[:rows, 0:H], in1=bfb.ap[:rows, :], op=ALU.add),
                           rd=[bk, bfb], wr=[zt])
                        op("act", lambda e: e.activation(out=zt.ap[:rows, H:2 * H], in_=zt.ap[:rows, 0:H], func=AF.Exp, scale=-1.0), rd=[zt], wr=[zt])
                        op("act", lambda e: e.activation(out=zt.ap[:rows, 0:H], in_=zt.ap[:rows, H:2 * H], func=AF.Ln, bias=ONE_T.ap[:rows, 0:1]),
                           rd=[zt, ONE_T], wr=[zt])
                        op("dve", lambda e: e.tensor_scalar(out=lfall.ap[:rows, ci, :], in0=zt.ap[:rows, 0:H], scalar1=-1.0, scalar2=None, op0=ALU.mult),
                           rd=[zt], wr=[lfall])
                        dma(lout[ro + s * 128:ro + s * 128 + rows, :], lfall.ap[:rows, ci, :], rd=[lfall], q="act")
                        cumsum_chunk(ci, rows, grp == "p" and ci == 0)
                b.barrier()
            if cfg.get("fox_stop") == "a":
                return
            with contextlib.ExitStack() as st:
                NPAIR_P = NCP * (NCP + 1) // 2
                biasall = b.sb(st, "a_bias", [128, NPAIR_P + NCS, H], F32)
                pair_idx = {}
                pi = 0
                for j in range(NCP):
                    for i in range(j + 1):
                        pair_idx[("p", j, i)] = pi
                        pi += 1
                for i in range(NCS):
                    pair_idx[("s", 0, i)] = pi
                    pi += 1
                for (g_, j, i), p_ in pair_idx.items():
                    if g_ == "p":
                        cj, ci_ = j, i
                    else:
                        cj, ci_ = NCP + NCS - 1, NCP + i
                    rw = TS if (g_ == "s" and i == NCS - 1) else 128
                    op("dve", lambda e: e.tensor_tensor(out=biasall.ap[:rw, p_, :], in0=cref.ap[:rw, cj, :], in1=c_all.ap[:rw, ci_, :], op=ALU.subtract),
                       rd=[cref, c_all], wr=[biasall])
                TMAX = max(T, P)
                qTh = [b.sb(st, "a_qTh%d" % i, [128, T], BF16) for i in range(2)]
                kTh = [b.sb(st, "a_kTh%d" % i, [128, TMAX + TS], BF16) for i in range(2)]
                Vh = [b.sb(st, "a_Vh%d" % i, [128, TMAX // 128 + 1, 128], BF16) for i in range(2)]
                oTh = [b.sb(st, "a_oTh%d" % i, [128, T], BF16) for i in range(2)]
                ckf = b.sb(st, "a_ckf", [128, P // 128, 128], F32)
                ckb = b.sb(st, "a_ckb", [128, P // 128, 128], BF16)
                cvf = b.sb(st, "a_cvf", [128, P // 128, 128], F32)
                pT = [b.sb(st, "a_pT%d" % i, [128, 128], BF16) for i in range(4)]
                rden = [b.sb(st, "a_rden%d" % i, [128, 128], F32) for i in range(2)]
                k_ = [0]
                scale = 128 ** -0.5

                ai = [0, 0]

                def abank():
                    ai[1] += 1
                    t = banks[4 + ai[1] % 4]
                    assert t.w is None or t.r, "PSUM bank re-allocated before its last result was read"
                    return t

                def attend(h, qsb, q0, nq, chunks, diag_i, pidx, osb, o0):
                    ai[0] += 1
                    numb, denb = banks[2 * (ai[0] % 2)], banks[2 * (ai[0] % 2) + 1]
                    n = len(chunks)
                    pend = []

                    def qk(i):
                        k_ap, v_ap, rows, rds = chunks[i]
                        sb_ = abank()
                        op("pe", lambda e: e.matmul(sb_.ap[:rows, :nq], lhsT=k_ap, rhs=qsb.ap[:, q0:q0 + nq], start=True, stop=True),
                           rd=rds + [qsb], wr=[sb_])
                        p_ = pT[k_[0] % 4]
                        k_[0] += 1
                        op("act", lambda e: e.activation(out=p_.ap[:rows, :nq], in_=sb_.ap[:rows, :nq], func=AF.Exp, scale=scale,
                                                         bias=biasall.ap[:rows, pidx[i], h:h + 1]), rd=[sb_, biasall], wr=[p_])
                        if i == diag_i:
                            op("dve", lambda e: e.tensor_tensor(out=p_.ap[:rows, :nq], in0=p_.ap[:rows, :nq], in1=tri_bf.ap[:rows, :nq], op=ALU.mult),
                               rd=[p_, tri_bf], wr=[p_])
                        pend.append((i, p_))

                    def pv():
                        i, p_ = pend.pop(0)
                        k_ap, v_ap, rows, rds = chunks[i]
                        op("pe", lambda e: e.matmul(numb.ap[:, :nq], lhsT=v_ap, rhs=p_.ap[:rows, :nq], start=(i == 0), stop=(i == n - 1)),
                           rd=rds + [p_], wr=[numb], sig=(i == n - 1))
                        op("pe", lambda e: e.matmul(denb.ap[:, :nq], lhsT=ones_bf.ap[:rows, :], rhs=p_.ap[:rows, :nq], start=(i == 0), stop=(i == n - 1)),
                           rd=[ones_bf, p_], wr=[denb])

                    for i in range(n):
                        qk(i)
                        if len(pend) > 2:
                            pv()
                    while pend:
                        pv()
                    r_ = rden[k_[0] % 2]
                    op("dve", lambda e: e.reciprocal(out=r_.ap[:, :nq], in_=denb.ap[:, :nq]), rd=[denb], wr=[r_])
                    op("dve", lambda e: e.tensor_tensor(out=osb.ap[:, o0:o0 + nq], in0=numb.ap[:, :nq], in1=r_.ap[:, :nq], op=ALU.mult),
                       rd=[numb, r_], wr=[osb])

                for h in range(H):
                    q_, k2, v_, o_ = qTh[h % 2], kTh[h % 2], Vh[h % 2], oTh[h % 2]
                    dma(q_.ap[:, :T], QT[h * 128:(h + 1) * 128, 0:T], wr=[q_])
                    dma(k2.ap[:, :T], KT[h * 128:(h + 1) * 128, 0:T], wr=[k2])
                    dma(v_.ap[:, 0:NCP, :], VV[0:T, h * 128:(h + 1) * 128].rearrange("(c p) d -> p c d", p=128), wr=[v_])
                    for j in range(NCP):
                        chunks = [(k2.ap[:, i * 128:(i + 1) * 128], v_.ap[:, i, :], 128, [k2, v_]) for i in range(j + 1)]
                        attend(h, q_, j * 128, 128, chunks, j, [pair_idx[("p", j, i)] for i in range(j + 1)], o_, j * 128)
                    dma(OT[h * 128:(h + 1) * 128, 0:T], o_.ap[:, :T], rd=[o_], q="act")
                for h in range(H):
                    q_, k2, v_, o_ = qTh[h % 2], kTh[h % 2], Vh[h % 2], oTh[h % 2]
                    dma(q_.ap[:, :TS], QT[h * 128:(h + 1) * 128, T:T + TS], wr=[q_])
                    dma(k2.ap[:, P:P + TS], KT[h * 128:(h + 1) * 128, T:T + TS], wr=[k2])
                    dma(v_.ap[:TS, P // 128, :], VV[T:T + TS, h * 128:(h + 1) * 128], wr=[v_])
                    dma(ckf.ap, I["cache_k_l1"][:, h * 128:(h + 1) * 128].rearrange("(c p) d -> p c d", p=128), wr=[ckf])
                    dma(cvf.ap, I["cache_v_l1"][:, h * 128:(h + 1) * 128].rearrange("(c p) d -> p c d", p=128), wr=[cvf])
                    op("dve", lambda e: e.tensor_copy(out=ckb.ap, in_=ckf.ap), rd=[ckf], wr=[ckb])
                    op("pool", lambda e: e.tensor_copy(out=v_.ap[:, 0:P // 128, :], in_=cvf.ap), rd=[cvf], wr=[v_])
                    for c0 in range(0, P // 128, 4):
                        tb = abank()
                        n4 = min(4, P // 128 - c0)
                        for j in range(n4):
                            op("pe", lambda e: e.transpose(out=bfv(tb)[:, j * 128:(j + 1) * 128], in_=ckb.ap[:, c0 + j, :], identity=ident_bf.ap),
                               rd=[ckb, ident_bf], wr=[tb], sig=(j == n4 - 1))
                        op("act", lambda e: e.copy(out=k2.ap[:, c0 * 128:(c0 + n4) * 128], in_=bfv(tb)[:, 0:n4 * 128]), rd=[tb], wr=[k2])
                    chunks = [(k2.ap[:, i * 128:(i + 1) * 128], v_.ap[:, i, :], 128, [k2, v_]) for i in range(P // 128)]
                    chunks.append((k2.ap[:, P:P + TS], v_.ap[:TS, P // 128, :], TS, [k2, v_]))
                    attend(h, q_, 0, TS, chunks, P // 128, [pair_idx[("s", 0, i)] for i in range(NCS)], o_, 0)
                    dma(OT[h * 128:(h + 1) * 128, T:T + TS], o_.ap[:, :TS], rd=[o_], q="act")
                b.barrier()
            if cfg.get("fox_stop") == "b":
                return
            with contextlib.ExitStack() as st:
                NS = (MT + 127) // 128
                xt = b.sb(st, "o_xt", [128, NS, D], F32)
                oT = b.sb(st, "o_oT", [128, KC, MT], BF16)
                for grp, r0, nt in tiles_of(MT):
                    load_x(xt, XB[r0:r0 + nt, :], nt)
                    dma(oT.ap[:, :, :nt], OT[:, r0:r0 + nt].rearrange("(c p) t -> p c t", p=128), wr=[oT])
                    proj_tm_add(oT, nt, I["fox_w_o_l1"], KC, xt)
                    store_x(xt, XA[r0:r0 + nt, :], nt)
                b.barrier()

    ONE_T = b.sb(gst, "one_t", [128, 1], F32)
    op("pool", lambda e: e.memset(ONE_T.ap, 1.0), wr=[ONE_T])

    def gla_phase(l):
        with contextlib.ExitStack() as st:
            M = MTG
            NS = (M + 127) // 128
            xt = b.sb(st, "g_xt", [128, NS, D], F32)
            hT = b.sb(st, "g_hT", [128, KC, M], BF16)
            qt_ = b.sb(st, "g_qt", [128, NDC, M], BF16)
            kt_ = b.sb(st, "g_kt", [128, NDC, M], BF16)
            kh_ = b.sb(st, "g_kh", [128, NDC, M], BF16)
            khT = [b.sb(st, "g_khT%d" % i, [128, 128], BF16) for i in range(8)]
            kTs = b.sb(st, "g_kTs", [128, M], F32)
            B16 = b.sb(st, "g_B16", [128, NDC, M], F32)
            tmpa = [b.sb(st, "g_tmpa%d" % i, [128, M], F32) for i in range(3)]
            eb = b.sb(st, "g_eb", [128, M], F32)
            enb = b.sb(st, "g_enb", [128, M], F32)
            ebl = b.sb(st, "g_ebl", [128, M], F32)
            nbl = b.sb(st, "g_nbl", [128, NDC, NS], F32)
            eblast = b.sb(st, "g_eblast", [128, NDC, NS], F32)
            rmask = b.sb(st, "g_rmask", [128, M], F32)
            srT = b.sb(st, "g_srT", [128, KC, M], BF16)
            vtm = b.sb(st, "g_vtm", [128, NS, D], BF16)
            gaT = b.sb(st, "g_gaT", [128, KC, M], BF16)
            oTs = b.sb(st, "g_oTs", [128, GH * VPH, M], F32)
            S = [b.sb(st, "g_S%d" % i, [128, DV], F32) for i in range(NDC)]
            Sb = [[b.sb(st, "g_Sb%d_%d" % (i, j), [128, DV], BF16) for j in range(2)] for i in range(NDC)]
            ATs = [b.sb(st, "g_ATs%d" % i, [128, 128], BF16) for i in range(4)]
            g1b = b.sb(st, "g_g1b", [16, M], BF16)
            wa1s = b.sb(st, "g_wa1s", [128, KC, 16], F32)
            wa1b = b.sb(st, "g_wa1b", [128, KC, 16], BF16)
            wa2s = b.sb(st, "g_wa2s", [16, QK], F32)
            wa2b = b.sb(st, "g_wa2b", [16, QK], BF16)
            rs2 = b.sb(st, "g_rs2", [128, M], F32)
            rs = b.sb(st, "g_rs", [128, M], F32)
            gT = load_gT(st, "g_gT", I["norm_mix_l%d" % l], KC)
            baT = load_gT(st, "g_baT", I["gla_b_a_l2"], NDC)
            nbaT = b.sb(st, "g_nbaT", [128, NDC], F32)
            goT = load_gT(st, "g_goT", I["gla_o_norm_l2"], VPH)
            W = I["gla_w_qkvr_l2"]
            op("dve", lambda e: e.tensor_scalar(out=nbaT.ap, in0=baT.ap, scalar1=-1.0, scalar2=None, op0=ALU.mult), rd=[baT], wr=[nbaT])
            dma(wa1s.ap, I["gla_w_a1_l2"].rearrange("(c p) r -> p c r", p=128), wr=[wa1s])
            op("dve", lambda e: e.tensor_copy(out=wa1b.ap, in_=wa1s.ap), rd=[wa1s], wr=[wa1b])
            dma(wa2s.ap, I["gla_w_a2_l2"], wr=[wa2s])
            op("dve", lambda e: e.tensor_copy(out=wa2b.ap, in_=wa2s.ap), rd=[wa2s], wr=[wa2b])
            op("pool", lambda e: e.memset(rmask.ap, 1.0), wr=[rmask])
            for s in range(NS):
                op("pool", lambda e: e.memset(rmask.ap[:, s * 128:s * 128 + 1], 0.0), wr=[rmask])
            k_ = [0]
            sbi = [0] * NDC
            tiles = tiles_of(M)
            for ti, (grp, r0, nt) in enumerate(tiles):
                last = (ti + 1 == len(tiles)) or tiles[ti + 1][0] != grp
                first = (ti == 0) or tiles[ti - 1][0] != grp
                subs = subs_of(nt)
                if first:
                    for dc in range(NDC):
                        if grp == "p":
                            op("pool", lambda e: e.memset(S[dc].ap, 0.0), wr=[S[dc]])
                        else:
                            dma(S[dc].ap, I["state_gla_l2"][dc * 128:(dc + 1) * 128, :], wr=[S[dc]])
                        sbi[dc] = 0
                        op("dve", lambda e: e.tensor_copy(out=Sb[dc][0].ap, in_=S[dc].ap), rd=[S[dc]], wr=[Sb[dc][0]])
                load_x(xt, XB[r0:r0 + nt, :], nt)
                norm_T(xt, nt, gT, hT)
                bk = bank()
                for kc in range(KC):
                    op("pe", lambda e: e.matmul(bk.ap[:16, :nt], lhsT=wa1b.ap[:, kc, :], rhs=hT.ap[:, kc, :nt], start=(kc == 0), stop=(kc == KC - 1)),
                       rd=[wa1b, hT], wr=[bk], sig=(kc == KC - 1))
                op("act", lambda e: e.copy(out=g1b.ap[:, :nt], in_=bk.ap[:16, :nt]), rd=[bk], wr=[g1b])
                for dc in range(NDC):
                    bk = bank()
                    op("pe", lambda e: e.matmul(bk.ap[:, :nt], lhsT=wa2b.ap[:, dc * 128:(dc + 1) * 128], rhs=g1b.ap[:, :nt], start=True, stop=True),
                       rd=[wa2b, g1b], wr=[bk])
                    t_ = tmpa[k_[0] % 3]
                    k_[0] += 1
                    op("act", lambda e: e.activation(out=t_.ap[:, :nt], in_=bk.ap[:, :nt], func=AF.Exp, scale=-1.0, bias=nbaT.ap[:, dc:dc + 1]),
                       rd=[bk, nbaT], wr=[t_])
                    op("act", lambda e: e.activation(out=t_.ap[:, :nt], in_=t_.ap[:, :nt], func=AF.Ln, bias=ONE_T.ap[:, 0:1]), rd=[t_, ONE_T], wr=[t_])
                    op("dve", lambda e: e.tensor_tensor_scan(out=B16.ap[:, dc, :nt], data0=rmask.ap[:, :nt], data1=t_.ap[:, :nt], initial=0.0,
                                                             op0=ALU.mult, op1=ALU.add), rd=[rmask, t_], wr=[B16])
                for s, rows in subs:
                    e_ = s * 128 + rows - 1
                    op("dve", lambda e: e.tensor_scalar(out=nbl.ap[:, :, s:s + 1], in0=B16.ap[:, :, e_:e_ + 1], scalar1=-1.0 / 16, scalar2=None, op0=ALU.mult),
                       rd=[B16], wr=[nbl])
                op("act", lambda e: e.activation(out=eblast.ap[:, :, 0:len(subs)], in_=nbl.ap[:, :, 0:len(subs)], func=AF.Exp), rd=[nbl], wr=[eblast])

                def cb_q(col0, bk):
                    dc = col0 // 128
                    op("act", lambda e: e.activation(out=eb.ap[:, :nt], in_=B16.ap[:, dc, :nt], func=AF.Exp, scale=-1.0 / 16), rd=[B16], wr=[eb])
                    op("dve", lambda e: e.scalar_tensor_tensor(out=qt_.ap[:, dc, :nt], in0=bk.ap[:, :nt], scalar=float(DK) ** -0.5, in1=eb.ap[:, :nt],
                                                               op0=ALU.mult, op1=ALU.mult), rd=[bk, eb], wr=[qt_])

                def cb_k(col0, bk):
                    dc = (col0 - QK) // 128
                    op("act", lambda e: e.copy(out=kTs.ap[:, :nt], in_=bk.ap[:, :nt]), rd=[bk], wr=[kTs])
                    op("act", lambda e: e.activation(out=enb.ap[:, :nt], in_=B16.ap[:, dc, :nt], func=AF.Exp, scale=1.0 / 16), rd=[B16], wr=[enb])
                    op("dve", lambda e: e.tensor_tensor(out=kt_.ap[:, dc, :nt], in0=kTs.ap[:, :nt], in1=enb.ap[:, :nt], op=ALU.mult), rd=[kTs, enb], wr=[kt_])
                    for s, rows in subs:
                        sl = slice(s * 128, s * 128 + rows)
                        op("act", lambda e: e.activation(out=ebl.ap[:, sl], in_=B16.ap[:, dc, sl], func=AF.Exp, scale=1.0 / 16, bias=nbl.ap[:, dc, s:s + 1]),
                           rd=[B16, nbl], wr=[ebl])
                    op("dve", lambda e: e.tensor_tensor(out=kh_.ap[:, dc, :nt], in0=kTs.ap[:, :nt], in1=ebl.ap[:, :nt], op=ALU.mult), rd=[kTs, ebl], wr=[kh_])

                proj_fm(hT, nt, W, list(range(0, QK, 256)), cb_q)
                proj_fm(hT, nt, W, list(range(QK, 2 * QK, 256)), cb_k)

                def cb_v(s, rows, cc, w, bk):
                    op("act", lambda e: e.copy(out=vtm.ap[:rows, s, cc:cc + w], in_=bk.ap[:rows, :w]), rd=[bk], wr=[vtm])
                proj_tm(hT, nt, W, 0, KC, 2 * QK, D, cb_v)

                def cb_r(col0, bk):
                    c = (col0 - 2 * QK - D) // 128
                    op("act", lambda e: e.activation(out=srT.ap[:, c, :nt], in_=bk.ap[:, :nt], func=AF.Silu), rd=[bk], wr=[srT])
                proj_fm(hT, nt, W, list(range(2 * QK + D, 2 * QK + 2 * D, 256)), cb_r)

                for s, rows in subs:
                    sl = slice(s * 128, s * 128 + rows)
                    for hh in range(GH):
                        dcs = [hh * EPH + e_i for e_i in range(EPH)]
                        ab = bank()
                        for n_, dc in enumerate(dcs):
                            op("pe", lambda e: e.matmul(ab.ap[:rows, :rows], lhsT=kt_.ap[:, dc, sl], rhs=qt_.ap[:, dc, sl], start=(n_ == 0), stop=(n_ == EPH - 1)),
                               rd=[kt_, qt_], wr=[ab], sig=(n_ == EPH - 1))
                        tbs = []
                        for dc in dcs:
                            tb = bank()
                            op("pe", lambda e: e.transpose(out=bfv(tb)[:rows, 0:128], in_=kh_.ap[:, dc, sl], identity=ident_bf.ap), rd=[kh_, ident_bf], wr=[tb])
                            tbs.append(tb)
                        at = ATs[k_[0] % 4]
                        op("dve", lambda e: e.tensor_tensor(out=at.ap[:rows, :rows], in0=ab.ap[:rows, :rows], in1=tri_f.ap[:rows, :rows], op=ALU.mult),
                           rd=[ab, tri_f], wr=[at])
                        kt2s = []
                        for tb in tbs:
                            kt2 = khT[k_[0] % 8]
                            k_[0] += 1
                            op("act", lambda e: e.copy(out=kt2.ap[:rows, :], in_=bfv(tb)[:rows, 0:128]), rd=[tb], wr=[kt2])
                            kt2s.append(kt2)
                        ubs = []
                        for kt2 in kt2s:
                            ub_ = bank()
                            op("pe", lambda e: e.matmul(ub_.ap[:, :DV], lhsT=kt2.ap[:rows, :], rhs=vtm.ap[:rows, s, hh * DV:(hh + 1) * DV], start=True, stop=True),
                               rd=[kt2, vtm], wr=[ub_])
                            ubs.append(ub_)
                        ob = bank()
                        for vc in range(VPH):
                            vsl = slice(hh * DV + vc * 128, hh * DV + (vc + 1) * 128)
                            op("pe", lambda e: e.matmul(ob.ap[:, vc * 128:vc * 128 + rows], lhsT=vtm.ap[:rows, s, vsl], rhs=at.ap[:rows, :rows], start=True, stop=False),
                               rd=[vtm, at], wr=[ob], sig=False)
                            for n_, dc in enumerate(dcs):
                                sbt = Sb[dc][sbi[dc] % 2]
                                op("pe", lambda e: e.matmul(ob.ap[:, vc * 128:vc * 128 + rows], lhsT=sbt.ap[:, vc * 128:(vc + 1) * 128], rhs=qt_.ap[:, dc, sl],
                                                            start=False, stop=(n_ == EPH - 1)), rd=[sbt, qt_], wr=[ob], sig=(n_ == EPH - 1 and vc == VPH - 1))
                        op("act", lambda e: e.copy(out=oTs.ap[:, hh * VPH:(hh + 1) * VPH, sl],
                                                   in_=ob.ap[:, 0:VPH * 128].rearrange("p (v t) -> p v t", v=VPH)[:, :, 0:rows]), rd=[ob], wr=[oTs])
                        for dc, ub_ in zip(dcs, ubs):
                            op("dve", lambda e: e.scalar_tensor_tensor(out=S[dc].ap, in0=S[dc].ap, scalar=eblast.ap[:, dc, s:s + 1], in1=ub_.ap[:, :DV],
                                                                       op0=ALU.mult, op1=ALU.add), rd=[S[dc], eblast, ub_], wr=[S[dc]])
                            sbi[dc] += 1
                            nb_ = Sb[dc][sbi[dc] % 2]
                            op("act", lambda e: e.copy(out=nb_.ap, in_=S[dc].ap), rd=[S[dc]], wr=[nb_])
                for hh in range(GH):
                    ssb = banks[7]
                    for vc in range(VPH):
                        t_ = tmpa[k_[0] % 3]
                        k_[0] += 1
                        op("act", lambda e: e.activation(out=t_.ap[:, :nt], in_=oTs.ap[:, hh * VPH + vc, :nt], func=AF.Square), rd=[oTs], wr=[t_])
                        op("pe", lambda e: e.matmul(ssb.ap[:, :nt], lhsT=ones_f.ap, rhs=t_.ap[:, :nt], start=(vc == 0), stop=(vc == VPH - 1)),
                           rd=[ones_f, t_], wr=[ssb])
                    op("act", lambda e: e.activation(out=rs2.ap[:, :nt], in_=ssb.ap[:, :nt], func=AF.Sqrt, scale=1.0 / DV, bias=EPS_T.ap[:, 0:1]),
                       rd=[ssb, EPS_T], wr=[rs2])
                    op("dve", lambda e: e.reciprocal(out=rs.ap[:, :nt], in_=rs2.ap[:, :nt]), rd=[rs2], wr=[rs])
                    for vc in range(VPH):
                        t_ = tmpa[k_[0] % 3]
                        k_[0] += 1
                        c = hh * VPH + vc
                        op("dve", lambda e: e.scalar_tensor_tensor(out=t_.ap[:, :nt], in0=oTs.ap[:, hh * VPH + vc, :nt], scalar=goT.ap[:, vc:vc + 1], in1=rs.ap[:, :nt],
                                                                   op0=ALU.mult, op1=ALU.mult), rd=[oTs, goT, rs], wr=[t_])
                        op("dve", lambda e: e.tensor_tensor(out=gaT.ap[:, c, :nt], in0=t_.ap[:, :nt], in1=srT.ap[:, c, :nt], op=ALU.mult),
                           rd=[t_, srT], wr=[gaT])
                if last:
                    for dc in range(NDC):
                        dma(O["gla_%s" % grp][dc * 128:(dc + 1) * 128, :], S[dc].ap, rd=[S[dc]], q="act")
                proj_tm_add(gaT, nt, I["gla_w_o_l2"], KC, xt)
                store_x(xt, XA[r0:r0 + nt, :], nt)
            b.barrier()

    b.barrier()
    ph = cfg.get("phases", "c0 f0 x1 f1 g2 f2 c3 f3").split()
    for p_ in ph:
        {"c": conv_phase, "f": ffn_phase, "x": fox_phase, "g": gla_phase}[p_[0]](int(p_[1]))
    b.barrier()
    es.close()
    return nc


OUT_ORDER = ["y_prompt", "y_sample", "conv0_p", "conv0_s", "k_p", "v_p", "lf_p", "k_s", "v_s", "lf_s", "gla_p", "gla_s", "conv3_p", "conv3_s"]
PER_BATCH = {"x_prompt": None, "x_sample": None, "cache_conv_l0": None, "cache_k_l1": "flat2", "cache_v_l1": "flat2",
             "cache_logf_l1": None, "state_gla_l2": "flat2h", "cache_conv_l3": None}


def run(inputs, cfg, n_cores):
    nc = build_program(cfg)
    D, T, TS, P = cfg["D"], cfg["T"], cfg["TS"], cfg["P"]
    H = D // 128
    in_maps = []
    shared = {k: np.ascontiguousarray(v, dtype=np.float32) for k, v in inputs.items() if k not in PER_BATCH}
    for c in range(n_cores):
        m = dict(shared)
        for k in PER_BATCH:
            a = np.asarray(inputs[k][c], dtype=np.float32)
            if k in ("cache_k_l1", "cache_v_l1"):
                a = a.reshape(P, D)
            elif k == "state_gla_l2":
                a = a.reshape(-1, a.shape[-1])
            m[k] = np.ascontiguousarray(a)
        in_maps.append(m)
    res = run_bass_kernel_spmd(nc, in_maps, core_ids=list(range(n_cores)))
    outs = []
    for name in OUT_ORDER:
        outs.append(np.stack([np.asarray(r[name]) for r in res.results], axis=0))
    o = dict(zip(OUT_ORDER, outs))
    B = n_cores
    DK, DV = D // 8, D // 4
    final = (o["y_prompt"], o["y_sample"], o["conv0_p"], o["conv0_s"],
             o["k_p"].reshape(B, T, H, 128), o["v_p"].reshape(B, T, H, 128), o["lf_p"],
             o["k_s"].reshape(B, TS, H, 128), o["v_s"].reshape(B, TS, H, 128), o["lf_s"],
             o["gla_p"].reshape(B, 4, DK, DV), o["gla_s"].reshape(B, 4, DK, DV), o["conv3_p"], o["conv3_s"])
    return tuple(np.ascontiguousarray(a, dtype=np.float32) for a in final)


def kernel(**inputs):
    return run(inputs, FULL, 8)
```

```python
import contextlib
import numpy as np
import concourse.bass as bass
import concourse.mybir as mybir
from concourse.bass_utils import run_bass_kernel_spmd

F32 = mybir.dt.float32
BF16 = mybir.dt.bfloat16
AF = mybir.ActivationFunctionType
ALU = mybir.AluOpType
AX = mybir.AxisListType

FULL = dict(D=2048, T=2048, TS=32, P=2048, DFF=8192, MT=512, MTG=256)
NDS = 40
WELEMS = 4096
CW = 31
CS = 30
EPS = 1e-6


INPUT_NAMES = (
    "x_prompt",
    "x_sample",
    "cache_conv_l0",
    "cache_k_l1",
    "cache_v_l1",
    "cache_logf_l1",
    "state_gla_l2",
    "cache_conv_l3",
    "norm_mix_l0",
    "conv_w_in_l0",
    "conv_w_dw_l0",
    "conv_norm_l0",
    "conv_w_out_l0",
    "norm_ffn_l0",
    "ffn_w_up_l0",
    "ffn_w_down_l0",
    "norm_mix_l1",
    "fox_w_qkv_l1",
    "fox_w_f_l1",
    "fox_b_f_l1",
    "fox_q_norm_l1",
    "fox_k_norm_l1",
    "fox_w_o_l1",
    "norm_ffn_l1",
    "ffn_w_up_l1",
    "ffn_w_down_l1",
    "norm_mix_l2",
    "gla_w_qkvr_l2",
    "gla_w_a1_l2",
    "gla_w_a2_l2",
    "gla_b_a_l2",
    "gla_o_norm_l2",
    "gla_w_o_l2",
    "norm_ffn_l2",
    "ffn_w_up_l2",
    "ffn_w_down_l2",
    "norm_mix_l3",
    "conv_w_in_l3",
    "conv_w_dw_l3",
    "conv_norm_l3",
    "conv_w_out_l3",
    "norm_ffn_l3",
    "ffn_w_up_l3",
    "ffn_w_down_l3",
)


class Tk:
    __slots__ = ("ap", "w", "r")

    def __init__(self, ap):
        self.ap = ap
        self.w = None
        self.r = {}


class Builder:
    def __init__(self, nc, es):
        self.nc = nc
        self.es = es
        self.engs = {"pe": nc.tensor, "act": nc.scalar, "dve": nc.vector, "pool": nc.gpsimd, "sp": nc.sync}
        self.sems = {}
        for k in self.engs:
            self.sems[k] = es.enter_context(nc.semaphore("s_" + k))
        for i in range(NDS):
            self.sems[("d", i)] = es.enter_context(nc.semaphore("d%d" % i))
        self.cnt = {k: 0 for k in self.sems}
        self.waited = {k: {} for k in self.engs}
        self.dma_i = 0
        self.rr = 0

    def wait(self, e, s, v):
        if self.waited[e].get(s, 0) >= v:
            return
        self.engs[e].wait_ge(self.sems[s], v)
        self.waited[e][s] = v

    def _deps(self, e, rd, wr, isdma=False):
        deps = {}

        def add(tok, raw):
            if tok is None:
                return
            s, v = tok
            if s == e and not isdma:
                if e == "pe":
                    return
            if deps.get(s, 0) < v:
                deps[s] = v

        for t in rd:
            add(t.w, True)
        for t in wr:
            add(t.w, False)
            for s, v in t.r.items():
                add((s, v), False)
        for s, v in deps.items():
            assert v <= self.cnt[s], ("wait on a not-yet-emitted signal", e, s, v, self.cnt[s])
            self.wait(e, s, v)

    def _mark(self, tok, rd, wr):
        s, v = tok
        for t in rd:
            if t.r.get(s, 0) < v:
                t.r[s] = v
        for t in wr:
            t.w = tok
            t.r = {}

    def op(self, e, fn, rd=(), wr=(), sig=True):
        self._deps(e, rd, wr)
        ins = fn(self.engs[e])
        if sig:
            self.cnt[e] += 1
            ins.then_inc(self.sems[e], 1)
            tok = (e, self.cnt[e])
        else:
            assert e == "pe"
            tok = (e, self.cnt[e] + 1)
        self._mark(tok, rd, wr)

    def dma(self, out, in_, rd=(), wr=(), q="sp", **kw):
        slot = ("d", self.dma_i % NDS)
        self.dma_i += 1
        self.cnt[slot] += 16
        tgt = self.cnt[slot]
        if tgt > 16:
            self.wait(q, slot, tgt - 16)
        self._deps(q, rd, wr, isdma=True)
        self.engs[q].dma_start(out=out, in_=in_, **kw).then_inc(self.sems[slot], 16)
        self._mark((slot, tgt), rd, wr)

    def barrier(self):
        for e in self.engs:
            for s, v in self.cnt.items():
                if s != e and v > 0:
                    self.wait(e, s, v)

    def sb(self, st, name, shape, dt=F32):
        self.rr += 1
        name = "%s_%d" % (name, self.rr)
        return Tk(st.enter_context(self.nc.sbuf_tensor(name, list(shape), dt)).ap())


def build_program(cfg):
    D, T, TS, P, DFF, MT, MTG = (cfg[k] for k in ("D", "T", "TS", "P", "DFF", "MT", "MTG"))
    KC = D // 128
    FC = DFF // 128
    H = D // 128
    GH = 4
    DK = D // 8
    DV = D // 4
    QK = GH * DK
    NDC = QK // 128
    EPH = DK // 128
    VPH = DV // 128
    R = T + TS
    assert T % MT == 0 and T % MTG == 0 and P % 128 == 0 and TS >= CS and TS <= 128

    nc = bass.Bass("TRN2", target_bir_lowering=False)
    es = contextlib.ExitStack()
    dt_in = lambda name, shape: nc.dram_tensor(name, list(shape), F32, kind="ExternalInput").ap()
    dt_out = lambda name, shape: nc.dram_tensor(name, list(shape), F32, kind="ExternalOutput").ap()
    I = {}
    I["x_prompt"] = dt_in("x_prompt", [T, D])
    I["x_sample"] = dt_in("x_sample", [TS, D])
    I["cache_conv_l0"] = dt_in("cache_conv_l0", [CS, D])
    I["cache_k_l1"] = dt_in("cache_k_l1", [P, D])
    I["cache_v_l1"] = dt_in("cache_v_l1", [P, D])
    I["cache_logf_l1"] = dt_in("cache_logf_l1", [P, H])
    I["state_gla_l2"] = dt_in("state_gla_l2", [QK, DV])
    I["cache_conv_l3"] = dt_in("cache_conv_l3", [CS, D])
    for l in range(4):
        I["norm_mix_l%d" % l] = dt_in("norm_mix_l%d" % l, [D])
        I["norm_ffn_l%d" % l] = dt_in("norm_ffn_l%d" % l, [D])
        I["ffn_w_up_l%d" % l] = dt_in("ffn_w_up_l%d" % l, [D, DFF])
        I["ffn_w_down_l%d" % l] = dt_in("ffn_w_down_l%d" % l, [DFF, D])
    for l in (0, 3):
        I["conv_w_in_l%d" % l] = dt_in("conv_w_in_l%d" % l, [D, 2 * D])
        I["conv_w_dw_l%d" % l] = dt_in("conv_w_dw_l%d" % l, [CW, D])
        I["conv_norm_l%d" % l] = dt_in("conv_norm_l%d" % l, [D])
        I["conv_w_out_l%d" % l] = dt_in("conv_w_out_l%d" % l, [D, D])
    I["fox_w_qkv_l1"] = dt_in("fox_w_qkv_l1", [D, 3 * D])
    I["fox_w_f_l1"] = dt_in("fox_w_f_l1", [D, H])
    I["fox_b_f_l1"] = dt_in("fox_b_f_l1", [H])
    I["fox_q_norm_l1"] = dt_in("fox_q_norm_l1", [128])
    I["fox_k_norm_l1"] = dt_in("fox_k_norm_l1", [128])
    I["fox_w_o_l1"] = dt_in("fox_w_o_l1", [D, D])
    I["gla_w_qkvr_l2"] = dt_in("gla_w_qkvr_l2", [D, 2 * QK + 2 * D])
    I["gla_w_a1_l2"] = dt_in("gla_w_a1_l2", [D, 16])
    I["gla_w_a2_l2"] = dt_in("gla_w_a2_l2", [16, QK])
    I["gla_b_a_l2"] = dt_in("gla_b_a_l2", [QK])
    I["gla_o_norm_l2"] = dt_in("gla_o_norm_l2", [DV])
    I["gla_w_o_l2"] = dt_in("gla_w_o_l2", [D, D])
    assert set(I) == set(INPUT_NAMES), set(I) ^ set(INPUT_NAMES)
    O = {}
    O["y_prompt"] = dt_out("y_prompt", [T, D])
    O["y_sample"] = dt_out("y_sample", [TS, D])
    O["conv0_p"] = dt_out("conv0_p", [CS, D])
    O["conv0_s"] = dt_out("conv0_s", [CS, D])
    O["k_p"] = dt_out("k_p", [T, D])
    O["v_p"] = dt_out("v_p", [T, D])
    O["lf_p"] = dt_out("lf_p", [T, H])
    O["k_s"] = dt_out("k_s", [TS, D])
    O["v_s"] = dt_out("v_s", [TS, D])
    O["lf_s"] = dt_out("lf_s", [TS, H])
    O["gla_p"] = dt_out("gla_p", [QK, DV])
    O["gla_s"] = dt_out("gla_s", [QK, DV])
    O["conv3_p"] = dt_out("conv3_p", [CS, D])
    O["conv3_s"] = dt_out("conv3_s", [CS, D])
    XA = nc.dram_tensor("XA", [R, D], F32).ap()
    XB = nc.dram_tensor("XB", [R, D], F32).ap()
    QT = nc.dram_tensor("QT", [D, R], BF16).ap()
    KT = nc.dram_tensor("KT", [D, R], BF16).ap()
    VV = nc.dram_tensor("VV", [R, D], BF16).ap()
    OT = nc.dram_tensor("OT", [D, R], BF16).ap()

    b = Builder(nc, es)
    op, dma = b.op, b.dma
    gst = es

    ident_bf = b.sb(gst, "ident_bf", [128, 128], BF16)
    ident_f = b.sb(gst, "ident_f", [128, 128], F32)
    ones_f = b.sb(gst, "ones_f", [128, 128], F32)
    ones_bf = b.sb(gst, "ones_bf", [128, 128], BF16)
    tri_f = b.sb(gst, "tri_f", [128, 128], F32)
    tri_bf = b.sb(gst, "tri_bf", [128, 128], BF16)
    for idt in (ident_bf, ident_f):
        op("pool", lambda e, t=idt: e.memset(t.ap, 0.0), wr=[idt])
        op("pool", lambda e, t=idt: e.affine_select(out=t.ap, in_=t.ap, compare_op=ALU.not_equal, fill=1.0, base=0,
                                                     pattern=[[-1, 128]], channel_multiplier=1), rd=[idt], wr=[idt])
    for o_ in (ones_f, ones_bf):
        op("pool", lambda e, t=o_: e.memset(t.ap, 1.0), wr=[o_])
    for tr in (tri_f, tri_bf):
        op("pool", lambda e, t=tr: e.memset(t.ap, 1.0), wr=[tr])
        op("pool", lambda e, t=tr: e.affine_select(out=t.ap, in_=t.ap, compare_op=ALU.is_ge, fill=0.0, base=0,
                                                    pattern=[[1, 128]], channel_multiplier=-1), rd=[tr], wr=[tr])
    banks = [Tk(es.enter_context(nc.psum_tensor("bank%d" % i, [128, 512], F32)).ap()) for i in range(8)]
    bank_i = [0]

    def bank():
        t = banks[bank_i[0] % 7]
        bank_i[0] += 1
        assert t.w is None or t.r, "PSUM bank re-allocated before its last result was read"
        return t

    def bfv(t):
        return t.ap.bitcast(BF16)

    NWS, NWB = 2, 3
    wst = [b.sb(gst, "wst%d" % i, [128, WELEMS], F32) for i in range(NWS)]
    wbf = [b.sb(gst, "wbf%d" % i, [128, WELEMS], BF16) for i in range(NWB)]
    wi = [0, 0]
    cast_rr = [0]
    CAST_ENGS = ["dve", "act"]
    wscr = {}
    WSCR_BLKS = cfg.get("wscr_blks", 32)

    def cast(dst_ap, src_ap, rd, wr, eng=None):
        if eng is None:
            eng = CAST_ENGS[cast_rr[0] % len(CAST_ENGS)]
            cast_rr[0] += 1
        if eng == "act":
            op("act", lambda e: e.copy(out=dst_ap, in_=src_ap), rd=rd, wr=wr)
        else:
            op(eng, lambda e: e.tensor_copy(out=dst_ap, in_=src_ap), rd=rd, wr=wr)

    def load_w(W, r0, nr, c0, ncol):
        G = nr // 128
        n = G * ncol
        assert n <= WELEMS
        wname = W.tensor.name
        if wname not in wscr:
            wscr[wname] = (nc.dram_tensor(wname + "_bfs", [WSCR_BLKS, 128, WELEMS], BF16).ap(), {})
        scr, blocks = wscr[wname]
        key = (r0, nr, c0, ncol)
        j = wi[1] % len(wbf)
        wi[1] += 1
        d_ap = wbf[j].ap[:, 0:n].rearrange("p (g n) -> p g n", g=G)
        if key in blocks:
            idx, btk = blocks[key]
            dma(wbf[j].ap[:, 0:n], scr[idx, :, 0:n], rd=[btk], wr=[wbf[j]])
            return wbf[j], d_ap
        i = wi[0] % NWS
        wi[0] += 1
        s_ap = wst[i].ap[:, 0:n].rearrange("p (g n) -> p g n", g=G)
        dma(s_ap, W[r0:r0 + nr, c0:c0 + ncol].rearrange("(g p) n -> p g n", p=128), wr=[wst[i]])
        cast(wbf[j].ap[:, 0:n], wst[i].ap[:, 0:n], rd=[wst[i]], wr=[wbf[j]])
        idx = len(blocks)
        assert idx < WSCR_BLKS, (wname, idx)
        btk = Tk(None)
        blocks[key] = (idx, btk)
        dma(scr[idx, :, 0:n], wbf[j].ap[:, 0:n], rd=[wbf[j]], wr=[btk], q="act")
        return wbf[j], d_ap

    @contextlib.contextmanager
    def extra_wbufs(st, n):
        for i in range(n):
            wbf.append(b.sb(st, "wbfx%d" % i, [128, WELEMS], BF16))
        try:
            yield
        finally:
            del wbf[NWB:]

    def subs_of(nt):
        return [(s, min(128, nt - 128 * s)) for s in range((nt + 127) // 128)]

    gT_cache = {}

    def load_gT(st, name, vec, n):
        key = vec.tensor.name
        if key not in gT_cache:
            t = b.sb(gst, name, [128, n], F32)
            with nc.allow_non_contiguous_dma(reason="tiny gain vector"):
                dma(t.ap, vec.rearrange("(c p) -> p c", p=128), wr=[t])
            gT_cache[key] = t
        return gT_cache[key]

    def load_bc(st, name, vec, n, reps=1):
        t = b.sb(st, name, [128, reps * n], F32)
        for r_ in range(reps):
            dma(t.ap[:, r_ * n:(r_ + 1) * n], vec.partition_broadcast(128), wr=[t])
        return t

    def load_x(xt, src, nt):
        for s, rows in subs_of(nt):
            dma(xt.ap[:rows, s, :], src[s * 128:s * 128 + rows, :], wr=[xt])

    def store_x(xt, dst, nt):
        for s, rows in subs_of(nt):
            dma(dst[s * 128:s * 128 + rows, :], xt.ap[:rows, s, :], rd=[xt], q="act")

    def rstd_from_ss(ss_ap, rd_t, tmp, out, n_inv, shape_sl):
        op("act", lambda e: e.activation(out=tmp.ap[shape_sl], in_=ss_ap, func=AF.Sqrt, scale=n_inv, bias=EPS_T.ap[shape_sl[0], 0:1]),
           rd=[rd_t, EPS_T], wr=[tmp])
        op("dve", lambda e: e.reciprocal(out=out.ap[shape_sl], in_=tmp.ap[shape_sl]), rd=[tmp], wr=[out])

    EPS_T = b.sb(gst, "eps_t", [128, 1], F32)
    op("pool", lambda e: e.memset(EPS_T.ap, EPS), wr=[EPS_T])
    junk = b.sb(gst, "junk", [128, D], BF16)
    hn = [b.sb(gst, "hn%d" % i, [128, D], BF16) for i in range(1)]
    nstat = b.sb(gst, "nstat", [128, 16], F32)
    nstat2 = b.sb(gst, "nstat2", [128, 16], F32)
    nstat3 = b.sb(gst, "nstat3", [128, 16], F32)
    hn_i = [0]

    def norm_T(xt, nt, gT, hT):
        for s, rows in subs_of(nt):
            op("act", lambda e: e.activation(out=junk.ap[:rows, :], in_=xt.ap[:rows, s, :], func=AF.Square,
                                             accum_out=nstat.ap[:rows, s:s + 1]), rd=[xt], wr=[junk, nstat])
            op("act", lambda e: e.activation(out=nstat2.ap[:rows, s:s + 1], in_=nstat.ap[:rows, s:s + 1], func=AF.Sqrt,
                                             scale=1.0 / D, bias=EPS_T.ap[:rows, 0:1]), rd=[nstat, EPS_T], wr=[nstat2])
            op("dve", lambda e: e.reciprocal(out=nstat3.ap[:rows, s:s + 1], in_=nstat2.ap[:rows, s:s + 1]), rd=[nstat2], wr=[nstat3])
            h_ = hn[0]
            hn_i[0] += 1
            op("act", lambda e: e.activation(out=h_.ap[:rows, :], in_=xt.ap[:rows, s, :], func=AF.Copy,
                                             scale=nstat3.ap[:rows, s:s + 1]), rd=[xt, nstat3], wr=[h_])
            for c0 in range(0, KC, 4):
                bk = bank()
                n4 = min(4, KC - c0)
                for j in range(n4):
                    c = c0 + j
                    op("pe", lambda e: e.transpose(out=bfv(bk)[:, j * 128:j * 128 + rows], in_=h_.ap[:rows, c * 128:(c + 1) * 128],
                                                   identity=ident_bf.ap[:rows, :rows]), rd=[h_, ident_bf], wr=[bk], sig=(j == n4 - 1))
                for j in range(n4):
                    c = c0 + j
                    eng = "dve" if j % 2 == 0 else "pool_no"
                    op("dve", lambda e: e.tensor_scalar(out=hT.ap[:, c, s * 128:s * 128 + rows], in0=bfv(bk)[:, j * 128:j * 128 + rows],
                                                        scalar1=gT.ap[:, c:c + 1], scalar2=None, op0=ALU.mult), rd=[bk, gT], wr=[hT])

    def proj_fm(hT, nt, W, blocks, cb):
        kin = W.shape[0] // 128
        for blk in blocks:
            wt, wap = load_w(W, 0, W.shape[0], blk, 256)
            for half in range(2):
                bk = bank()
                for kc in range(kin):
                    op("pe", lambda e: e.matmul(bk.ap[:, :nt], lhsT=wap[:, kc, half * 128:(half + 1) * 128], rhs=hT.ap[:, kc, :nt],
                                                start=(kc == 0), stop=(kc == kin - 1)), rd=[wt, hT], wr=[bk], sig=(kc == kin - 1))
                cb(blk + half * 128, bk)

    def proj_tm(aT, nt, W, r0, kin, c0, ncols, cb, hook=None, cbb=None):
        subs = subs_of(nt)
        for cc in range(0, ncols, 512):
            w = min(512, ncols - cc)
            bks = [bank() for _ in subs]
            G = max(1, min(kin, WELEMS // w))
            for kg in range(0, kin, G):
                g_n = min(G, kin - kg)
                wt, wap = load_w(W, r0 + kg * 128, g_n * 128, c0 + cc, w)
                for g in range(g_n):
                    k = kg + g
                    for s, rows in subs:
                        op("pe", lambda e: e.matmul(bks[s].ap[:rows, :w], lhsT=aT.ap[:, k, s * 128:s * 128 + rows], rhs=wap[:, g, :w],
                                                    start=(k == 0), stop=(k == kin - 1)), rd=[wt, aT], wr=[bks[s]],
                           sig=(k == kin - 1 or (g == g_n - 1 and s == subs[-1][0])))
            if cbb is not None:
                cbb(cc, w, [(s, rows, bks[s]) for s, rows in subs])
            else:
                for s, rows in subs:
                    cb(s, rows, cc, w, bks[s])
            if hook is not None:
                hook()

    def proj_tm_add(aT, nt, W, kin, xt):
        def cb(s, rows, cc, w, bk):
            op("dve", lambda e: e.tensor_tensor(out=xt.ap[:rows, s, cc:cc + w], in0=bk.ap[:rows, :w], in1=xt.ap[:rows, s, cc:cc + w],
                                                op=ALU.add), rd=[bk, xt], wr=[xt])
        proj_tm(aT, nt, W, 0, kin, 0, D, cb)

    def transpose_f32_rows(src, n, dst, dst_sl):
        for c in range(KC):
            bk = bank()
            op("pe", lambda e: e.transpose(out=bk.ap[:, :n], in_=src.ap[:n, c * 128:(c + 1) * 128], identity=ident_f.ap[:n, :n]),
               rd=[src, ident_f], wr=[bk])
            op("dve", lambda e: e.tensor_copy(out=dst.ap[:, c, dst_sl], in_=bk.ap[:, :n]), rd=[bk], wr=[dst])

    def tiles_of(mt):
        lst = [("p", r0, mt) for r0 in range(0, T, mt)]
        lst.append(("s", T, TS))
        return lst

    def xsrc(l, grp, r0, nt):
        if l == 0:
            return I["x_prompt"][r0:r0 + nt, :] if grp == "p" else I["x_sample"][0:nt, :]
        return XB[r0:r0 + nt, :]

    def ydst(l, grp, r0, nt):
        if l == 3:
            return O["y_prompt"][r0:r0 + nt, :] if grp == "p" else O["y_sample"][0:nt, :]
        return XB[r0:r0 + nt, :]

    def ffn_phase(l):
        with contextlib.ExitStack() as st:
            NS = (MT + 127) // 128
            xt = b.sb(st, "f_xt", [128, NS, D], F32)
            hT = b.sb(st, "f_hT", [128, KC, MT], BF16)
            aT = b.sb(st, "f_aT", [128, FC, MT], BF16)
            sq = [b.sb(st, "f_sq%d" % i, [128, MT], F32) for i in range(2)]
            gT = load_gT(st, "f_gT", I["norm_ffn_l%d" % l], KC)
            Wu, Wd = I["ffn_w_up_l%d" % l], I["ffn_w_down_l%d" % l]
            st.enter_context(extra_wbufs(st, cfg.get("xw_ffn", 3)))
            k_ = [0]
            for grp, r0, nt in tiles_of(MT):
                load_x(xt, XA[r0:r0 + nt, :], nt)
                norm_T(xt, nt, gT, hT)

                def cb(col0, bk):
                    fc = col0 // 128
                    s_ = sq[k_[0] % 2]
                    k_[0] += 1
                    op("act", lambda e: e.activation(out=s_.ap[:, :nt], in_=bk.ap[:, :nt], func=AF.Square), rd=[bk], wr=[s_])
                    op("dve", lambda e: e.scalar_tensor_tensor(out=aT.ap[:, fc, :nt], in0=bk.ap[:, :nt], scalar=0.0, in1=s_.ap[:, :nt],
                                                               op0=ALU.is_gt, op1=ALU.mult), rd=[bk, s_], wr=[aT])
                proj_fm(hT, nt, Wu, list(range(0, DFF, 256)), cb)
                proj_tm_add(aT, nt, Wd, FC, xt)
                store_x(xt, ydst(l, grp, r0, nt), nt)
            b.barrier()

    def conv_phase(l):
        with contextlib.ExitStack() as st:
            NS = (MT + 127) // 128
            xt = b.sb(st, "c_xt", [128, NS, D], F32)
            hT = b.sb(st, "c_hT", [128, KC, MT], BF16)
            ub = b.sb(st, "c_ub", [128, KC, CS + MT], BF16)
            ul = b.sb(st, "c_ul", [128, KC, CS], F32)
            a_sb = b.sb(st, "c_a", [128, 2, MT], F32)
            sg = [b.sb(st, "c_sg%d" % i, [128, MT], F32) for i in range(2)]
            ycp = b.sb(st, "c_y", [128, KC, MT], F32)
            zT = b.sb(st, "c_zT", [128, KC, MT], BF16)
            rsb = b.sb(st, "c_rsb", [128, MT], F32)
            rsb2 = b.sb(st, "c_rsb2", [128, MT], F32)
            dg = [b.sb(st, "c_dg%d" % i, [128, 128], BF16) for i in range(12)]
            wdw_tm = b.sb(st, "c_wdwtm", [32, D], F32)
            wdwT = b.sb(st, "c_wdwT", [128, KC, CW], F32)
            hist_tm = wdw_tm
            st_tm = wdw_tm
            gT = load_gT(st, "c_gT", I["norm_mix_l%d" % l], KC)
            gcT = load_gT(st, "c_gcT", I["conv_norm_l%d" % l], KC)
            Win, Wout = I["conv_w_in_l%d" % l], I["conv_w_out_l%d" % l]
            dma(wdw_tm.ap[:CW, :], I["conv_w_dw_l%d" % l], wr=[wdw_tm])
            transpose_f32_rows(wdw_tm, CW, wdwT, slice(0, CW))
            tiles = tiles_of(MT)
            k_ = [0]
            for ti, (grp, r0, nt) in enumerate(tiles):
                last = (ti + 1 == len(tiles)) or tiles[ti + 1][0] != grp
                first = (ti == 0) or tiles[ti - 1][0] != grp
                load_x(xt, xsrc(l, grp, r0, nt), nt)
                norm_T(xt, nt, gT, hT)
                if grp == "p" and first:
                    op("pool", lambda e: e.memset(ub.ap[:, :, 0:CS], 0.0), wr=[ub])
                elif grp == "p":
                    op("dve", lambda e: e.tensor_copy(out=ub.ap[:, :, 0:CS], in_=ub.ap[:, :, pnt:pnt + CS]), rd=[ub], wr=[ub])
                else:
                    dma(hist_tm.ap[:CS, :], I["cache_conv_l%d" % l], wr=[hist_tm])
                    transpose_f32_rows(hist_tm, CS, ub, slice(0, CS))
                pnt = nt

                for cb2 in range(0, D, 256):
                    def cb_a(col0, bk):
                        j = (col0 - cb2) // 128
                        op("act", lambda e: e.copy(out=a_sb.ap[:, j, :nt], in_=bk.ap[:, :nt]), rd=[bk], wr=[a_sb])
                    proj_fm(hT, nt, Win, [cb2], cb_a)

                    def cb_b(col0, bk):
                        j = (col0 - D - cb2) // 128
                        c = (col0 - D) // 128
                        s_ = sg[k_[0] % 2]
                        k_[0] += 1
                        op("act", lambda e: e.activation(out=s_.ap[:, :nt], in_=bk.ap[:, :nt], func=AF.Sigmoid), rd=[bk], wr=[s_])
                        op("dve", lambda e: e.tensor_tensor(out=ub.ap[:, c, CS:CS + nt], in0=a_sb.ap[:, j, :nt], in1=s_.ap[:, :nt], op=ALU.mult),
                           rd=[a_sb, s_], wr=[ub])
                        if last:
                            op("dve", lambda e: e.tensor_tensor(out=ul.ap[:, c, :], in0=a_sb.ap[:, j, nt - CS:nt], in1=s_.ap[:, nt - CS:nt],
                                                                op=ALU.mult), rd=[a_sb, s_], wr=[ul])
                    proj_fm(hT, nt, Win, [D + cb2], cb_b)
                if last:
                    for c in range(KC):
                        bk = bank()
                        op("pe", lambda e: e.transpose(out=bk.ap[:CS, :128], in_=ul.ap[:, c, :], identity=ident_f.ap), rd=[ul, ident_f], wr=[bk])
                        op("dve", lambda e: e.tensor_copy(out=st_tm.ap[:CS, c * 128:(c + 1) * 128], in_=bk.ap[:CS, :128]), rd=[bk], wr=[st_tm])
                    dma(O["conv%d_%s" % (l, grp)], st_tm.ap[:CS, :], rd=[st_tm], q="act")
                ssb = banks[7]
                for c in range(KC):
                    bk = bank()
                    for j in range(CW):
                        d_ = dg[k_[0] % 12]
                        k_[0] += 1
                        if k_[0] % 2 == 0:
                            op("dve", lambda e: e.tensor_scalar(out=d_.ap, in0=ident_bf.ap, scalar1=wdwT.ap[:, c, j:j + 1], scalar2=None, op0=ALU.mult),
                               rd=[ident_bf, wdwT], wr=[d_])
                        else:
                            op("act", lambda e: e.activation(out=d_.ap, in_=ident_bf.ap, func=AF.Copy, scale=wdwT.ap[:, c, j:j + 1]),
                               rd=[ident_bf, wdwT], wr=[d_])
                        op("pe", lambda e: e.matmul(bk.ap[:, :nt], lhsT=d_.ap, rhs=ub.ap[:, c, j:j + nt], start=(j == 0), stop=(j == CW - 1)),
                           rd=[d_, ub], wr=[bk])
                    op("act", lambda e: e.copy(out=ycp.ap[:, c, :nt], in_=bk.ap[:, :nt]), rd=[bk], wr=[ycp])
                    s_ = sg[k_[0] % 2]
                    k_[0] += 1
                    op("act", lambda e: e.activation(out=s_.ap[:, :nt], in_=bk.ap[:, :nt], func=AF.Square), rd=[bk], wr=[s_])
                    op("pe", lambda e: e.matmul(ssb.ap[:, :nt], lhsT=ones_f.ap, rhs=s_.ap[:, :nt], start=(c == 0), stop=(c == KC - 1)),
                       rd=[ones_f, s_], wr=[ssb])
                op("act", lambda e: e.activation(out=rsb2.ap[:, :nt], in_=ssb.ap[:, :nt], func=AF.Sqrt, scale=1.0 / D, bias=EPS_T.ap[:, 0:1]),
                   rd=[ssb, EPS_T], wr=[rsb2])
                op("dve", lambda e: e.reciprocal(out=rsb.ap[:, :nt], in_=rsb2.ap[:, :nt]), rd=[rsb2], wr=[rsb])
                for c in range(KC):
                    s_ = sg[k_[0] % 2]
                    k_[0] += 1
                    op("dve", lambda e: e.scalar_tensor_tensor(out=s_.ap[:, :nt], in0=ycp.ap[:, c, :nt], scalar=gcT.ap[:, c:c + 1], in1=rsb.ap[:, :nt],
                                                               op0=ALU.mult, op1=ALU.mult), rd=[ycp, gcT, rsb], wr=[s_])
                    op("act", lambda e: e.activation(out=zT.ap[:, c, :nt], in_=s_.ap[:, :nt], func=AF.Silu), rd=[s_], wr=[zT])
                proj_tm_add(zT, nt, Wout, KC, xt)
                store_x(xt, XA[r0:r0 + nt, :], nt)
            b.barrier()

    def fox_phase(l):
        NCP = T // 128
        NCS = P // 128 + 1
        with contextlib.ExitStack() as st0:
            c_all = b.sb(st0, "x_call", [128, NCP + NCS, H], F32)
            cref = b.sb(st0, "x_cref", [128, NCP + NCS, H], F32)
            with contextlib.ExitStack() as st:
                NS = (MT + 127) // 128
                xt = b.sb(st, "x_xt", [128, NS, D], F32)
                hT = b.sb(st, "x_hT", [128, KC, MT], BF16)
                qTm = b.sb(st, "x_qTm", [128, H, MT], BF16)
                kTm = b.sb(st, "x_kTm", [128, H, MT], BF16)
                sqt = [b.sb(st, "x_sqt%d" % i, [128, 512], F32) for i in range(4)]
                t1 = [b.sb(st, "x_t1%d" % i, [128, 512], F32) for i in range(4)]
                stg = [b.sb(st, "x_stg%d" % i, [128, 512], F32) for i in range(4)]
                nb = [b.sb(st, "x_nb%d" % i, [128, 512], BF16) for i in range(8)]
                defer = []
                defer_old = []

                def flush_defer():
                    while defer_old:
                        defer_old.pop(0)()
                    defer_old.extend(defer)
                    del defer[:]
                s4 = [b.sb(st, "x_s4%d" % i, [128, 12], F32) for i in range(4)]
                lfall = b.sb(st, "x_lfall", [128, NCP + NCS, H], F32)
                zt = b.sb(st, "x_zt", [128, 2 * H], F32)
                wf_st = b.sb(st, "x_wfst", [128, KC, H], F32)
                wfb = b.sb(st, "x_wfb", [128, KC, H], BF16)
                bfb = load_bc(st, "x_bfb", I["fox_b_f_l1"], H)
                gqb = load_bc(st, "x_gqb", I["fox_q_norm_l1"], 128, reps=4)
                gkb = load_bc(st, "x_gkb", I["fox_k_norm_l1"], 128, reps=4)
                gT = load_gT(st, "x_gT", I["norm_mix_l%d" % l], KC)
                carry = b.sb(st, "x_carry", [128, H], F32)
                Wqkv = I["fox_w_qkv_l1"]
                dma(wf_st.ap, I["fox_w_f_l1"].rearrange("(c p) h -> p c h", p=128), wr=[wf_st])
                op("dve", lambda e: e.tensor_copy(out=wfb.ap, in_=wf_st.ap), rd=[wf_st], wr=[wfb])
                k_ = [0]

                def cumsum_chunk(ci, rows, first):
                    bk = bank()
                    op("pe", lambda e: e.matmul(bk.ap[:rows, 0:H], lhsT=tri_f.ap[:rows, :rows], rhs=lfall.ap[:rows, ci, :], start=True, stop=True),
                       rd=[tri_f, lfall], wr=[bk])
                    bk2 = bank()
                    op("pe", lambda e: e.matmul(bk2.ap[:, 0:H], lhsT=ones_f.ap[:rows, :], rhs=lfall.ap[:rows, ci, :], start=True, stop=True),
                       rd=[ones_f, lfall], wr=[bk2])
                    if first:
                        op("dve", lambda e: e.tensor_copy(out=c_all.ap[:rows, ci, :], in_=bk.ap[:rows, 0:H]), rd=[bk], wr=[c_all])
                        op("dve", lambda e: e.tensor_copy(out=carry.ap, in_=bk2.ap[:, 0:H]), rd=[bk2], wr=[carry])
                    else:
                        op("dve", lambda e: e.tensor_tensor(out=c_all.ap[:rows, ci, :], in0=bk.ap[:rows, 0:H], in1=carry.ap[:rows, :], op=ALU.add),
                           rd=[bk, carry], wr=[c_all])
                        op("dve", lambda e: e.tensor_tensor(out=carry.ap, in0=bk2.ap[:, 0:H], in1=carry.ap, op=ALU.add), rd=[bk2, carry], wr=[carry])
                    op("dve", lambda e: e.tensor_copy(out=cref.ap[:, ci, :], in_=carry.ap), rd=[carry], wr=[cref])

                for grp, r0, nt in tiles_of(MT):
                    load_x(xt, XB[r0:r0 + nt, :], nt)
                    norm_T(xt, nt, gT, hT)
                    kout = O["k_p"] if grp == "p" else O["k_s"]
                    vout = O["v_p"] if grp == "p" else O["v_s"]
                    lout = O["lf_p"] if grp == "p" else O["lf_s"]
                    ro = r0 if grp == "p" else 0

                    def cb_qk(which):
                        gb_ = gqb if which == "q" else gkb
                        dstT = qTm if which == "q" else kTm

                        def cb(cc, w, items):
                            nh = w // 128
                            idx = []
                            for (s, rows, bk) in items:
                                idx.append((k_[0] % 4, k_[0] % 8))
                                k_[0] += 1
                            for (s, rows, bk), (i4, i3) in zip(items, idx):
                                op("act", lambda e: e.activation(out=sqt[i4].ap[:rows, :w], in_=bk.ap[:rows, :w], func=AF.Square), rd=[bk], wr=[sqt[i4]])
                            for (s, rows, bk), (i4, i3) in zip(items, idx):
                                op("dve", lambda e: e.tensor_tensor(out=t1[i4].ap[:rows, :w], in0=bk.ap[:rows, :w], in1=gb_.ap[:rows, :w], op=ALU.mult),
                                   rd=[bk, gb_, sqt[i4]], wr=[t1[i4]])
                            for (s, rows, bk), (i4, i3) in zip(items, idx):
                                op("dve", lambda e: e.tensor_reduce(out=s4[i4].ap[:rows, 0:nh], in_=sqt[i4].ap[:rows, :w].rearrange("p (h d) -> p h d", h=nh),
                                                                    axis=AX.X, op=ALU.add), rd=[sqt[i4]], wr=[s4[i4]])
                            for (s, rows, bk), (i4, i3) in zip(items, idx):
                                op("act", lambda e: e.activation(out=s4[i4].ap[:rows, 4:4 + nh], in_=s4[i4].ap[:rows, 0:nh], func=AF.Sqrt, scale=1.0 / 128,
                                                                 bias=EPS_T.ap[:rows, 0:1]), rd=[s4[i4], EPS_T], wr=[s4[i4]])
                            for (s, rows, bk), (i4, i3) in zip(items, idx):
                                op("dve", lambda e: e.reciprocal(out=s4[i4].ap[:rows, 8:8 + nh], in_=s4[i4].ap[:rows, 4:4 + nh]), rd=[s4[i4]], wr=[s4[i4]])
                            for (s, rows, bk), (i4, i3) in zip(items, idx):
                                for hh in range(nh):
                                    sl = slice(hh * 128, (hh + 1) * 128)
                                    dst_ = stg[i4] if which == "k" else nb[i3]
                                    op("dve", lambda e: e.tensor_scalar(out=dst_.ap[:rows, sl], in0=t1[i4].ap[:rows, sl],
                                                                        scalar1=s4[i4].ap[:rows, 8 + hh:9 + hh], scalar2=None, op0=ALU.mult),
                                       rd=[t1[i4], s4[i4]], wr=[dst_])
                            for (s, rows, bk), (i4, i3) in zip(items, idx):
                                if which == "k":
                                    dma(kout[ro + s * 128:ro + s * 128 + rows, cc:cc + w], stg[i4].ap[:rows, :w], rd=[stg[i4]], q="act")
                                    op("act", lambda e: e.copy(out=nb[i3].ap[:rows, :w], in_=stg[i4].ap[:rows, :w]), rd=[stg[i4]], wr=[nb[i3]])

                                def part_b(i3=i3, rows=rows, s=s, cc=cc, nh=nh, dstT=dstT):
                                    tb = bank()
                                    for hh in range(nh):
                                        op("pe", lambda e: e.transpose(out=bfv(tb)[:, hh * 128:hh * 128 + rows], in_=nb[i3].ap[:rows, hh * 128:(hh + 1) * 128],
                                                                       identity=ident_bf.ap[:rows, :rows]), rd=[nb[i3], ident_bf], wr=[tb], sig=(hh == nh - 1))
                                    h0 = cc // 128
                                    op("act", lambda e: e.copy(out=dstT.ap[:, h0:h0 + nh, s * 128:s * 128 + rows],
                                                               in_=bfv(tb)[:, 0:nh * 128].rearrange("p (h t) -> p h t", h=nh)[:, :, 0:rows]), rd=[tb], wr=[dstT])
                                defer.append(part_b)
                        return cb

                    def cb_v(s, rows, cc, w, bk):
                        i3 = k_[0] % 8
                        i4 = k_[0] % 4
                        k_[0] += 1
                        op("act", lambda e: e.copy(out=stg[i4].ap[:rows, :w], in_=bk.ap[:rows, :w]), rd=[bk], wr=[stg[i4]])
                        op("dve", lambda e: e.tensor_copy(out=nb[i3].ap[:rows, :w], in_=stg[i4].ap[:rows, :w]), rd=[stg[i4]], wr=[nb[i3]])
                        dma(vout[ro + s * 128:ro + s * 128 + rows, cc:cc + w], stg[i4].ap[:rows, :w], rd=[stg[i4]], q="act")
                        dma(VV[r0 + s * 128:r0 + s * 128 + rows, cc:cc + w], nb[i3].ap[:rows, :w], rd=[nb[i3]], q="act")

                    proj_tm(hT, nt, Wqkv, 0, KC, 0, D, None, hook=flush_defer, cbb=cb_qk("q"))
                    proj_tm(hT, nt, Wqkv, 0, KC, D, D, None, hook=flush_defer, cbb=cb_qk("k"))
                    proj_tm(hT, nt, Wqkv, 0, KC, 2 * D, D, cb_v, hook=flush_defer)
                    flush_defer()
                    flush_defer()
                    dma(QT[:, r0:r0 + nt].rearrange("(h p) t -> p h t", p=128), qTm.ap[:, :, :nt], rd=[qTm], q="act")
                    dma(KT[:, r0:r0 + nt].rearrange("(h p) t -> p h t", p=128), kTm.ap[:, :, :nt], rd=[kTm], q="act")
                    if grp == "s":
                        dma(lfall.ap[:, NCP:NCP + P // 128, :], I["cache_logf_l1"].rearrange("(c p) h -> p c h", p=128), wr=[lfall])
                        for ci in range(P // 128):
                            cumsum_chunk(NCP + ci, 128, ci == 0)
                    for s, rows in subs_of(nt):
                        bk = bank()
                        for kc in range(KC):
                            op("pe", lambda e: e.matmul(bk.ap[:rows, 0:H], lhsT=hT.ap[:, kc, s * 128:s * 128 + rows], rhs=wfb.ap[:, kc, :],
                                                        start=(kc == 0), stop=(kc == KC - 1)), rd=[hT, wfb], wr=[bk], sig=(kc == KC - 1))
                        ci = (r0 // 128 + s) if grp == "p" else (NCP + P // 128)
                        op("dve", lambda e: e.tensor_tensor(out=zt.ap[:rows, 0:H], in0=bk.ap[:rows, 0:H], in1=bfb.ap[:rows, :], op=ALU.add),
                           rd=[bk, bfb], wr=[zt])
                        op("act", lambda e: e.activation(out=zt.ap[:rows, H:2 * H], in_=zt.ap[:rows, 0:H], func=AF.Exp, scale=-1.0), rd=[zt], wr=[zt])
                        op("act", lambda e: e.activation(out=zt.ap[:rows, 0:H], in_=zt.ap[:rows, H:2 * H], func=AF.Ln, bias=ONE_T.ap[:rows, 0:1]),
                           rd=[zt, ONE_T], wr=[zt])
                        op("dve", lambda e: e.tensor_scalar(out=lfall.ap[:rows, ci, :], in0=zt.ap[:rows, 0:H], scalar1=-1.0, scalar2=None, op0=ALU.mult),
                           rd=[zt], wr=[lfall])
                        dma(lout[ro + s * 128:ro + s * 128 + rows, :], lfall.ap[:rows, ci, :], rd=[lfall], q="act")
                        cumsum_chunk(ci, rows, grp == "p" and ci == 0)
                b.barrier()
            if cfg.get("fox_stop") == "a":
                return
            with contextlib.ExitStack() as st:
                NPAIR_P = NCP * (NCP + 1) // 2
                biasall = b.sb(st, "a_bias", [128, NPAIR_P + NCS, H], F32)
                pair_idx = {}
                pi = 0
                for j in range(NCP):
                    for i in range(j + 1):
                        pair_idx[("p", j, i)] = pi
                        pi += 1
                for i in range(NCS):
                    pair_idx[("s", 0, i)] = pi
                    pi += 1
                for (g_, j, i), p_ in pair_idx.items():
                    if g_ == "p":
                        cj, ci_ = j, i
                    else:
                        cj, ci_ = NCP + NCS - 1, NCP + i
                    rw = TS if (g_ == "s" and i == NCS - 1) else 128
                    op("dve", lambda e: e.tensor_tensor(out=biasall.ap[:rw, p_, :], in0=cref.ap[:rw, cj, :], in1=c_all.ap[:rw, ci_, :], op=ALU.subtract),
                       rd=[cref, c_all], wr=[biasall])
                TMAX = max(T, P)
                qTh = [b.sb(st, "a_qTh%d" % i, [128, T], BF16) for i in range(2)]
                kTh = [b.sb(st, "a_kTh%d" % i, [128, TMAX + TS], BF16) for i in range(2)]
                Vh = [b.sb(st, "a_Vh%d" % i, [128, TMAX // 128 + 1, 128], BF16) for i in range(2)]
                oTh = [b.sb(st, "a_oTh%d" % i, [128, T], BF16) for i in range(2)]
                ckf = b.sb(st, "a_ckf", [128, P // 128, 128], F32)
                ckb = b.sb(st, "a_ckb", [128, P // 128, 128], BF16)
                cvf = b.sb(st, "a_cvf", [128, P // 128, 128], F32)
                pT = [b.sb(st, "a_pT%d" % i, [128, 128], BF16) for i in range(4)]
                rden = [b.sb(st, "a_rden%d" % i, [128, 128], F32) for i in range(2)]
                k_ = [0]
                scale = 128 ** -0.5

                ai = [0, 0]

                def abank():
                    ai[1] += 1
                    t = banks[4 + ai[1] % 4]
                    assert t.w is None or t.r, "PSUM bank re-allocated before its last result was read"
                    return t

                def attend(h, qsb, q0, nq, chunks, diag_i, pidx, osb, o0):
                    ai[0] += 1
                    numb, denb = banks[2 * (ai[0] % 2)], banks[2 * (ai[0] % 2) + 1]
                    n = len(chunks)
                    pend = []

                    def qk(i):
                        k_ap, v_ap, rows, rds = chunks[i]
                        sb_ = abank()
                        op("pe", lambda e: e.matmul(sb_.ap[:rows, :nq], lhsT=k_ap, rhs=qsb.ap[:, q0:q0 + nq], start=True, stop=True),
                           rd=rds + [qsb], wr=[sb_])
                        p_ = pT[k_[0] % 4]
                        k_[0] += 1
                        op("act", lambda e: e.activation(out=p_.ap[:rows, :nq], in_=sb_.ap[:rows, :nq], func=AF.Exp, scale=scale,
                                                         bias=biasall.ap[:rows, pidx[i], h:h + 1]), rd=[sb_, biasall], wr=[p_])
                        if i == diag_i:
                            op("dve", lambda e: e.tensor_tensor(out=p_.ap[:rows, :nq], in0=p_.ap[:rows, :nq], in1=tri_bf.ap[:rows, :nq], op=ALU.mult),
                               rd=[p_, tri_bf], wr=[p_])
                        pend.append((i, p_))

                    def pv():
                        i, p_ = pend.pop(0)
                        k_ap, v_ap, rows, rds = chunks[i]
                        op("pe", lambda e: e.matmul(numb.ap[:, :nq], lhsT=v_ap, rhs=p_.ap[:rows, :nq], start=(i == 0), stop=(i == n - 1)),
                           rd=rds + [p_], wr=[numb], sig=(i == n - 1))
                        op("pe", lambda e: e.matmul(denb.ap[:, :nq], lhsT=ones_bf.ap[:rows, :], rhs=p_.ap[:rows, :nq], start=(i == 0), stop=(i == n - 1)),
                           rd=[ones_bf, p_], wr=[denb])

                    for i in range(n):
                        qk(i)
                        if len(pend) > 2:
                            pv()
                    while pend:
                        pv()
                    r_ = rden[k_[0] % 2]
                    op("dve", lambda e: e.reciprocal(out=r_.ap[:, :nq], in_=denb.ap[:, :nq]), rd=[denb], wr=[r_])
                    op("dve", lambda e: e.tensor_tensor(out=osb.ap[:, o0:o0 + nq], in0=numb.ap[:, :nq], in1=r_.ap[:, :nq], op=ALU.mult),
                       rd=[numb, r_], wr=[osb])

                for h in range(H):
                    q_, k2, v_, o_ = qTh[h % 2], kTh[h % 2], Vh[h % 2], oTh[h % 2]
                    dma(q_.ap[:, :T], QT[h * 128:(h + 1) * 128, 0:T], wr=[q_])
                    dma(k2.ap[:, :T], KT[h * 128:(h + 1) * 128, 0:T], wr=[k2])
                    dma(v_.ap[:, 0:NCP, :], VV[0:T, h * 128:(h + 1) * 128].rearrange("(c p) d -> p c d", p=128), wr=[v_])
                    for j in range(NCP):
                        chunks = [(k2.ap[:, i * 128:(i + 1) * 128], v_.ap[:, i, :], 128, [k2, v_]) for i in range(j + 1)]
                        attend(h, q_, j * 128, 128, chunks, j, [pair_idx[("p", j, i)] for i in range(j + 1)], o_, j * 128)
                    dma(OT[h * 128:(h + 1) * 128, 0:T], o_.ap[:, :T], rd=[o_], q="act")
                for h in range(H):
                    q_, k2, v_, o_ = qTh[h % 2], kTh[h % 2], Vh[h % 2], oTh[h % 2]
                    dma(q_.ap[:, :TS], QT[h * 128:(h + 1) * 128, T:T + TS], wr=[q_])
                    dma(k2.ap[:, P:P + TS], KT[h * 128:(h + 1) * 128, T:T + TS], wr=[k2])
                    dma(v_.ap[:TS, P // 128, :], VV[T:T + TS, h * 128:(h + 1) * 128], wr=[v_])
                    dma(ckf.ap, I["cache_k_l1"][:, h * 128:(h + 1) * 128].rearrange("(c p) d -> p c d", p=128), wr=[ckf])
                    dma(cvf.ap, I["cache_v_l1"][:, h * 128:(h + 1) * 128].rearrange("(c p) d -> p c d", p=128), wr=[cvf])
                    op("dve", lambda e: e.tensor_copy(out=ckb.ap, in_=ckf.ap), rd=[ckf], wr=[ckb])
                    op("pool", lambda e: e.tensor_copy(out=v_.ap[:, 0:P // 128, :], in_=cvf.ap), rd=[cvf], wr=[v_])
                    for c0 in range(0, P // 128, 4):
                        tb = abank()
                        n4 = min(4, P // 128 - c0)
                        for j in range(n4):
                            op("pe", lambda e: e.transpose(out=bfv(tb)[:, j * 128:(j + 1) * 128], in_=ckb.ap[:, c0 + j, :], identity=ident_bf.ap),
                               rd=[ckb, ident_bf], wr=[tb], sig=(j == n4 - 1))
                        op("act", lambda e: e.copy(out=k2.ap[:, c0 * 128:(c0 + n4) * 128], in_=bfv(tb)[:, 0:n4 * 128]), rd=[tb], wr=[k2])
                    chunks = [(k2.ap[:, i * 128:(i + 1) * 128], v_.ap[:, i, :], 128, [k2, v_]) for i in range(P // 128)]
                    chunks.append((k2.ap[:, P:P + TS], v_.ap[:TS, P // 128, :], TS, [k2, v_]))
                    attend(h, q_, 0, TS, chunks, P // 128, [pair_idx[("s", 0, i)] for i in range(NCS)], o_, 0)
                    dma(OT[h * 128:(h + 1) * 128, T:T + TS], o_.ap[:, :TS], rd=[o_], q="act")
                b.barrier()
            if cfg.get("fox_stop") == "b":
                return
            with contextlib.ExitStack() as st:
                NS = (MT + 127) // 128
                xt = b.sb(st, "o_xt", [128, NS, D], F32)
                oT = b.sb(st, "o_oT", [128, KC, MT], BF16)
                for grp, r0, nt in tiles_of(MT):
                    load_x(xt, XB[r0:r0 + nt, :], nt)
                    dma(oT.ap[:, :, :nt], OT[:, r0:r0 + nt].rearrange("(c p) t -> p c t", p=128), wr=[oT])
                    proj_tm_add(oT, nt, I["fox_w_o_l1"], KC, xt)
                    store_x(xt, XA[r0:r0 + nt, :], nt)
                b.barrier()

    ONE_T = b.sb(gst, "one_t", [128, 1], F32)
    op("pool", lambda e: e.memset(ONE_T.ap, 1.0), wr=[ONE_T])

    def gla_phase(l):
        with contextlib.ExitStack() as st:
            M = MTG
            NS = (M + 127) // 128
            xt = b.sb(st, "g_xt", [128, NS, D], F32)
            hT = b.sb(st, "g_hT", [128, KC, M], BF16)
            qt_ = b.sb(st, "g_qt", [128, NDC, M], BF16)
            kt_ = b.sb(st, "g_kt", [128, NDC, M], BF16)
            kh_ = b.sb(st, "g_kh", [128, NDC, M], BF16)
            khT = [b.sb(st, "g_khT%d" % i, [128, 128], BF16) for i in range(8)]
            kTs = b.sb(st, "g_kTs", [128, M], F32)
            B16 = b.sb(st, "g_B16", [128, NDC, M], F32)
            tmpa = [b.sb(st, "g_tmpa%d" % i, [128, M], F32) for i in range(3)]
            eb = b.sb(st, "g_eb", [128, M], F32)
            enb = b.sb(st, "g_enb", [128, M], F32)
            ebl = b.sb(st, "g_ebl", [128, M], F32)
            nbl = b.sb(st, "g_nbl", [128, NDC, NS], F32)
            eblast = b.sb(st, "g_eblast", [128, NDC, NS], F32)
            rmask = b.sb(st, "g_rmask", [128, M], F32)
            srT = b.sb(st, "g_srT", [128, KC, M], BF16)
            vtm = b.sb(st, "g_vtm", [128, NS, D], BF16)
            gaT = b.sb(st, "g_gaT", [128, KC, M], BF16)
            oTs = b.sb(st, "g_oTs", [128, GH * VPH, M], F32)
            S = [b.sb(st, "g_S%d" % i, [128, DV], F32) for i in range(NDC)]
            Sb = [[b.sb(st, "g_Sb%d_%d" % (i, j), [128, DV], BF16) for j in range(2)] for i in range(NDC)]
            ATs = [b.sb(st, "g_ATs%d" % i, [128, 128], BF16) for i in range(4)]
            g1b = b.sb(st, "g_g1b", [16, M], BF16)
            wa1s = b.sb(st, "g_wa1s", [128, KC, 16], F32)
            wa1b = b.sb(st, "g_wa1b", [128, KC, 16], BF16)
            wa2s = b.sb(st, "g_wa2s", [16, QK], F32)
            wa2b = b.sb(st, "g_wa2b", [16, QK], BF16)
            rs2 = b.sb(st, "g_rs2", [128, M], F32)
            rs = b.sb(st, "g_rs", [128, M], F32)
            gT = load_gT(st, "g_gT", I["norm_mix_l%d" % l], KC)
            baT = load_gT(st, "g_baT", I["gla_b_a_l2"], NDC)
            nbaT = b.sb(st, "g_nbaT", [128, NDC], F32)
            goT = load_gT(st, "g_goT", I["gla_o_norm_l2"], VPH)
            W = I["gla_w_qkvr_l2"]
            op("dve", lambda e: e.tensor_scalar(out=nbaT.ap, in0=baT.ap, scalar1=-1.0, scalar2=None, op0=ALU.mult), rd=[baT], wr=[nbaT])
            dma(wa1s.ap, I["gla_w_a1_l2"].rearrange("(c p) r -> p c r", p=128), wr=[wa1s])
            op("dve", lambda e: e.tensor_copy(out=wa1b.ap, in_=wa1s.ap), rd=[wa1s], wr=[wa1b])
            dma(wa2s.ap, I["gla_w_a2_l2"], wr=[wa2s])
            op("dve", lambda e: e.tensor_copy(out=wa2b.ap, in_=wa2s.ap), rd=[wa2s], wr=[wa2b])
            op("pool", lambda e: e.memset(rmask.ap, 1.0), wr=[rmask])
            for s in range(NS):
                op("pool", lambda e: e.memset(rmask.ap[:, s * 128:s * 128 + 1], 0.0), wr=[rmask])
            k_ = [0]
            sbi = [0] * NDC
            tiles = tiles_of(M)
            for ti, (grp, r0, nt) in enumerate(tiles):
                last = (ti + 1 == len(tiles)) or tiles[ti + 1][0] != grp
                first = (ti == 0) or tiles[ti - 1][0] != grp
                subs = subs_of(nt)
                if first:
                    for dc in range(NDC):
                        if grp == "p":
                            op("pool", lambda e: e.memset(S[dc].ap, 0.0), wr=[S[dc]])
                        else:
                            dma(S[dc].ap, I["state_gla_l2"][dc * 128:(dc + 1) * 128, :], wr=[S[dc]])
                        sbi[dc] = 0
                        op("dve", lambda e: e.tensor_copy(out=Sb[dc][0].ap, in_=S[dc].ap), rd=[S[dc]], wr=[Sb[dc][0]])
                load_x(xt, XB[r0:r0 + nt, :], nt)
                norm_T(xt, nt, gT, hT)
                bk = bank()
                for kc in range(KC):
                    op("pe", lambda e: e.matmul(bk.ap[:16, :nt], lhsT=wa1b.ap[:, kc, :], rhs=hT.ap[:, kc, :nt], start=(kc == 0), stop=(kc == KC - 1)),
                       rd=[wa1b, hT], wr=[bk], sig=(kc == KC - 1))
                op("act", lambda e: e.copy(out=g1b.ap[:, :nt], in_=bk.ap[:16, :nt]), rd=[bk], wr=[g1b])
                for dc in range(NDC):
                    bk = bank()
                    op("pe", lambda e: e.matmul(bk.ap[:, :nt], lhsT=wa2b.ap[:, dc * 128:(dc + 1) * 128], rhs=g1b.ap[:, :nt], start=True, stop=True),
                       rd=[wa2b, g1b], wr=[bk])
                    t_ = tmpa[k_[0] % 3]
                    k_[0] += 1
                    op("act", lambda e: e.activation(out=t_.ap[:, :nt], in_=bk.ap[:, :nt], func=AF.Exp, scale=-1.0, bias=nbaT.ap[:, dc:dc + 1]),
                       rd=[bk, nbaT], wr=[t_])
                    op("act", lambda e: e.activation(out=t_.ap[:, :nt], in_=t_.ap[:, :nt], func=AF.Ln, bias=ONE_T.ap[:, 0:1]), rd=[t_, ONE_T], wr=[t_])
                    op("dve", lambda e: e.tensor_tensor_scan(out=B16.ap[:, dc, :nt], data0=rmask.ap[:, :nt], data1=t_.ap[:, :nt], initial=0.0,
                                                             op0=ALU.mult, op1=ALU.add), rd=[rmask, t_], wr=[B16])
                for s, rows in subs:
                    e_ = s * 128 + rows - 1
                    op("dve", lambda e: e.tensor_scalar(out=nbl.ap[:, :, s:s + 1], in0=B16.ap[:, :, e_:e_ + 1], scalar1=-1.0 / 16, scalar2=None, op0=ALU.mult),
                       rd=[B16], wr=[nbl])
                op("act", lambda e: e.activation(out=eblast.ap[:, :, 0:len(subs)], in_=nbl.ap[:, :, 0:len(subs)], func=AF.Exp), rd=[nbl], wr=[eblast])

                def cb_q(col0, bk):
                    dc = col0 // 128
                    op("act", lambda e: e.activation(out=eb.ap[:, :nt], in_=B16.ap[:, dc, :nt], func=AF.Exp, scale=-1.0 / 16), rd=[B16], wr=[eb])
                    op("dve", lambda e: e.scalar_tensor_tensor(out=qt_.ap[:, dc, :nt], in0=bk.ap[:, :nt], scalar=float(DK) ** -0.5, in1=eb.ap[:, :nt],
                                                               op0=ALU.mult, op1=ALU.mult), rd=[bk, eb], wr=[qt_])

                def cb_k(col0, bk):
                    dc = (col0 - QK) // 128
                    op("act", lambda e: e.copy(out=kTs.ap[:, :nt], in_=bk.ap[:, :nt]), rd=[bk], wr=[kTs])
                    op("act", lambda e: e.activation(out=enb.ap[:, :nt], in_=B16.ap[:, dc, :nt], func=AF.Exp, scale=1.0 / 16), rd=[B16], wr=[enb])
                    op("dve", lambda e: e.tensor_tensor(out=kt_.ap[:, dc, :nt], in0=kTs.ap[:, :nt], in1=enb.ap[:, :nt], op=ALU.mult), rd=[kTs, enb], wr=[kt_])
                    for s, rows in subs:
                        sl = slice(s * 128, s * 128 + rows)
                        op("act", lambda e: e.activation(out=ebl.ap[:, sl], in_=B16.ap[:, dc, sl], func=AF.Exp, scale=1.0 / 16, bias=nbl.ap[:, dc, s:s + 1]),
                           rd=[B16, nbl], wr=[ebl])
                    op("dve", lambda e: e.tensor_tensor(out=kh_.ap[:, dc, :nt], in0=kTs.ap[:, :nt], in1=ebl.ap[:, :nt], op=ALU.mult), rd=[kTs, ebl], wr=[kh_])

                proj_fm(hT, nt, W, list(range(0, QK, 256)), cb_q)
                proj_fm(hT, nt, W, list(range(QK, 2 * QK, 256)), cb_k)

                def cb_v(s, rows, cc, w, bk):
                    op("act", lambda e: e.copy(out=vtm.ap[:rows, s, cc:cc + w], in_=bk.ap[:rows, :w]), rd=[bk], wr=[vtm])
                proj_tm(hT, nt, W, 0, KC, 2 * QK, D, cb_v)

                def cb_r(col0, bk):
                    c = (col0 - 2 * QK - D) // 128
                    op("act", lambda e: e.activation(out=srT.ap[:, c, :nt], in_=bk.ap[:, :nt], func=AF.Silu), rd=[bk], wr=[srT])
                proj_fm(hT, nt, W, list(range(2 * QK + D, 2 * QK + 2 * D, 256)), cb_r)

                for s, rows in subs:
                    sl = slice(s * 128, s * 128 + rows)
                    for hh in range(GH):
                        dcs = [hh * EPH + e_i for e_i in range(EPH)]
                        ab = bank()
                        for n_, dc in enumerate(dcs):
                            op("pe", lambda e: e.matmul(ab.ap[:rows, :rows], lhsT=kt_.ap[:, dc, sl], rhs=qt_.ap[:, dc, sl], start=(n_ == 0), stop=(n_ == EPH - 1)),
                               rd=[kt_, qt_], wr=[ab], sig=(n_ == EPH - 1))
                        tbs = []
                        for dc in dcs:
                            tb = bank()
                            op("pe", lambda e: e.transpose(out=bfv(tb)[:rows, 0:128], in_=kh_.ap[:, dc, sl], identity=ident_bf.ap), rd=[kh_, ident_bf], wr=[tb])
                            tbs.append(tb)
                        at = ATs[k_[0] % 4]
                        op("dve", lambda e: e.tensor_tensor(out=at.ap[:rows, :rows], in0=ab.ap[:rows, :rows], in1=tri_f.ap[:rows, :rows], op=ALU.mult),
                           rd=[ab, tri_f], wr=[at])
                        kt2s = []
                        for tb in tbs:
                            kt2 = khT[k_[0] % 8]
                            k_[0] += 1
                            op("act", lambda e: e.copy(out=kt2.ap[:rows, :], in_=bfv(tb)[:rows, 0:128]), rd=[tb], wr=[kt2])
                            kt2s.append(kt2)
                        ubs = []
                        for kt2 in kt2s:
                            ub_ = bank()
                            op("pe", lambda e: e.matmul(ub_.ap[:, :DV], lhsT=kt2.ap[:rows, :], rhs=vtm.ap[:rows, s, hh * DV:(hh + 1) * DV], start=True, stop=True),
                               rd=[kt2, vtm], wr=[ub_])
                            ubs.append(ub_)
                        ob = bank()
                        for vc in range(VPH):
                            vsl = slice(hh * DV + vc * 128, hh * DV + (vc + 1) * 128)
                            op("pe", lambda e: e.matmul(ob.ap[:, vc * 128:vc * 128 + rows], lhsT=vtm.ap[:rows, s, vsl], rhs=at.ap[:rows, :rows], start=True, stop=False),
                               rd=[vtm, at], wr=[ob], sig=False)
                            for n_, dc in enumerate(dcs):
                                sbt = Sb[dc][sbi[dc] % 2]
                                op("pe", lambda e: e.matmul(ob.ap[:, vc * 128:vc * 128 + rows], lhsT=sbt.ap[:, vc * 128:(vc + 1) * 128], rhs=qt_.ap[:, dc, sl],
                                                            start=False, stop=(n_ == EPH - 1)), rd=[sbt, qt_], wr=[ob], sig=(n_ == EPH - 1 and vc == VPH - 1))
                        op("act", lambda e: e.copy(out=oTs.ap[:, hh * VPH:(hh + 1) * VPH, sl],
                                                   in_=ob.ap[:, 0:VPH * 128].rearrange("p (v t) -> p v t", v=VPH)[:, :, 0:rows]), rd=[ob], wr=[oTs])
                        for dc, ub_ in zip(dcs, ubs):
                            op("dve", lambda e: e.scalar_tensor_tensor(out=S[dc].ap, in0=S[dc].ap, scalar=eblast.ap[:, dc, s:s + 1], in1=ub_.ap[:, :DV],
                                                                       op0=ALU.mult, op1=ALU.add), rd=[S[dc], eblast, ub_], wr=[S[dc]])
                            sbi[dc] += 1
                            nb_ = Sb[dc][sbi[dc] % 2]
                            op("act", lambda e: e.copy(out=nb_.ap, in_=S[dc].ap), rd=[S[dc]], wr=[nb_])
                for hh in range(GH):
                    ssb = banks[7]
                    for vc in range(VPH):
                        t_ = tmpa[k_[0] % 3]
                        k_[0] += 1
                        op("act", lambda e: e.activation(out=t_.ap[:, :nt], in_=oTs.ap[:, hh * VPH + vc, :nt], func=AF.Square), rd=[oTs], wr=[t_])
                        op("pe", lambda e: e.matmul(ssb.ap[:, :nt], lhsT=ones_f.ap, rhs=t_.ap[:, :nt], start=(vc == 0), stop=(vc == VPH - 1)),
                           rd=[ones_f, t_], wr=[ssb])
                    op("act", lambda e: e.activation(out=rs2.ap[:, :nt], in_=ssb.ap[:, :nt], func=AF.Sqrt, scale=1.0 / DV, bias=EPS_T.ap[:, 0:1]),
                       rd=[ssb, EPS_T], wr=[rs2])
                    op("dve", lambda e: e.reciprocal(out=rs.ap[:, :nt], in_=rs2.ap[:, :nt]), rd=[rs2], wr=[rs])
                    for vc in range(VPH):
                        t_ = tmpa[k_[0] % 3]
                        k_[0] += 1
                        c = hh * VPH + vc
                        op("dve", lambda e: e.scalar_tensor_tensor(out=t_.ap[:, :nt], in0=oTs.ap[:, hh * VPH + vc, :nt], scalar=goT.ap[:, vc:vc + 1], in1=rs.ap[:, :nt],
                                                                   op0=ALU.mult, op1=ALU.mult), rd=[oTs, goT, rs], wr=[t_])
                        op("dve", lambda e: e.tensor_tensor(out=gaT.ap[:, c, :nt], in0=t_.ap[:, :nt], in1=srT.ap[:, c, :nt], op=ALU.mult),
                           rd=[t_, srT], wr=[gaT])
                if last:
                    for dc in range(NDC):
                        dma(O["gla_%s" % grp][dc * 128:(dc + 1) * 128, :], S[dc].ap, rd=[S[dc]], q="act")
                proj_tm_add(gaT, nt, I["gla_w_o_l2"], KC, xt)
                store_x(xt, XA[r0:r0 + nt, :], nt)
            b.barrier()

    for l_ in range(4):
        load_gT(gst, "gmix%d" % l_, I["norm_mix_l%d" % l_], KC)
        load_gT(gst, "gffn%d" % l_, I["norm_ffn_l%d" % l_], KC)
    for l_ in (0, 3):
        load_gT(gst, "gconv%d" % l_, I["conv_norm_l%d" % l_], KC)
    load_gT(gst, "gba", I["gla_b_a_l2"], NDC)
    load_gT(gst, "ggo", I["gla_o_norm_l2"], VPH)
    b.barrier()
    ph = cfg.get("phases", "c0 f0 x1 f1 g2 f2 c3 f3").split()
    for p_ in ph:
        {"c": conv_phase, "f": ffn_phase, "x": fox_phase, "g": gla_phase}[p_[0]](int(p_[1]))
    b.barrier()
    es.close()
    return nc


OUT_ORDER = ["y_prompt", "y_sample", "conv0_p", "conv0_s", "k_p", "v_p", "lf_p", "k_s", "v_s", "lf_s", "gla_p", "gla_s", "conv3_p", "conv3_s"]
PER_BATCH = {"x_prompt": None, "x_sample": None, "cache_conv_l0": None, "cache_k_l1": "flat2", "cache_v_l1": "flat2",
             "cache_logf_l1": None, "state_gla_l2": "flat2h", "cache_conv_l3": None}


def run(inputs, cfg, n_cores):
    nc = build_program(cfg)
    D, T, TS, P = cfg["D"], cfg["T"], cfg["TS"], cfg["P"]
    H = D // 128
    in_maps = []
    shared = {k: np.ascontiguousarray(v, dtype=np.float32) for k, v in inputs.items() if k not in PER_BATCH}
    for c in range(n_cores):
        m = dict(shared)
        for k in PER_BATCH:
            a = np.asarray(inputs[k][c], dtype=np.float32)
            if k in ("cache_k_l1", "cache_v_l1"):
                a = a.reshape(P, D)
            elif k == "state_gla_l2":
                a = a.reshape(-1, a.shape[-1])
            m[k] = np.ascontiguousarray(a)
        in_maps.append(m)
    res = run_bass_kernel_spmd(nc, in_maps, core_ids=list(range(n_cores)))
    outs = []
    for name in OUT_ORDER:
        outs.append(np.stack([np.asarray(r[name]) for r in res.results], axis=0))
    o = dict(zip(OUT_ORDER, outs))
    B = n_cores
    DK, DV = D // 8, D // 4
    final = (o["y_prompt"], o["y_sample"], o["conv0_p"], o["conv0_s"],
             o["k_p"].reshape(B, T, H, 128), o["v_p"].reshape(B, T, H, 128), o["lf_p"],
             o["k_s"].reshape(B, TS, H, 128), o["v_s"].reshape(B, TS, H, 128), o["lf_s"],
             o["gla_p"].reshape(B, 4, DK, DV), o["gla_s"].reshape(B, 4, DK, DV), o["conv3_p"], o["conv3_s"])
    return tuple(np.ascontiguousarray(a, dtype=np.float32) for a in final)


def kernel(**inputs):
    return run(inputs, FULL, 8)
```
